# Optimizing a Trainium2 kernel written in Bass

```python
import jax, jax.numpy as jnp
from jax import lax
import numpy as np

D_MODEL = 4096
BATCH = 4
SEQ = 2048
DEPTH = 1
DEC_BATCH = 32
DEC_SEQ = 1
PAST_LEN = 8192
PAGE_SIZE = 128

HEAD_DIM = 128
H_A = 12
KVH_A = 4
H_IDX = 16
D_IDX = 64
TOPK_MAX = 256
H_B = 12
KVH_B = 4
H_C = 4
HD_C = 256
N_MEM = 256
ROPE_THETA = 500000.0
ROT_FRAC = 4
Q_BLOCK = 128
EPS = 1e-6

W_A = H_A * HEAD_DIM
W_B = H_B * HEAD_DIM
W_C = H_C * HD_C
IN_SIZES = (W_A, KVH_A * HEAD_DIM, KVH_A * HEAD_DIM, H_IDX * D_IDX, D_IDX, H_IDX, W_A,
            W_B, KVH_B * HEAD_DIM, KVH_B * HEAD_DIM, H_B, W_B,
            W_C, W_C, 3 * D_MODEL)
D_IN = sum(IN_SIZES)

kernel_name = 'hybrid_dsa_fox_memxattn_gated_step'


def _split_points():
    pts, acc = [], 0
    for size in IN_SIZES[:-1]:
        acc += size
        pts.append(acc)
    return pts


def _rmsnorm(x, g):
    xf = x.astype(jnp.float32)
    y = xf * lax.rsqrt(jnp.mean(xf * xf, axis=-1, keepdims=True) + EPS)
    return (y * g.astype(jnp.float32)).astype(x.dtype)


def _rotary(x, pos):
    rot = x.shape[-1] // ROT_FRAC
    half = rot // 2
    inv = jnp.power(jnp.float32(ROPE_THETA), -jnp.arange(half, dtype=jnp.float32) * (2.0 / rot))
    ang = pos.astype(jnp.float32)[:, None] * inv[None, :]
    cos = jnp.cos(ang)[None, :, None, :]
    sin = jnp.sin(ang)[None, :, None, :]
    xf = x.astype(jnp.float32)
    x1, x2, rest = xf[..., :half], xf[..., half:rot], xf[..., rot:]
    return jnp.concatenate([x1 * cos - x2 * sin, x1 * sin + x2 * cos, rest], axis=-1).astype(x.dtype)


def _project(x, pos, g_norm, w_in, b_forget):
    b, t, _ = x.shape
    xn = _rmsnorm(x, g_norm)
    (qa, ka, va, iq, ik, iw, za, qb, kb, vb, fb, zb, qc, zc, gates) = jnp.split(xn @ w_in, _split_points(), axis=-1)
    qa = _rotary(qa.reshape(b, t, H_A, HEAD_DIM), pos)
    ka = _rotary(ka.reshape(b, t, KVH_A, HEAD_DIM), pos)
    va = va.reshape(b, t, KVH_A, HEAD_DIM)
    iq = _rotary(iq.reshape(b, t, H_IDX, D_IDX), pos)
    ik = _rotary(ik.reshape(b, t, 1, D_IDX), pos)[:, :, 0]
    qb = qb.reshape(b, t, H_B, HEAD_DIM)
    kb = kb.reshape(b, t, KVH_B, HEAD_DIM)
    vb = vb.reshape(b, t, KVH_B, HEAD_DIM)
    logf = jax.nn.log_sigmoid(fb.astype(jnp.float32) + b_forget.astype(jnp.float32)).astype(x.dtype)
    qc = qc.reshape(b, t, H_C, HD_C)
    return qa, ka, va, iq, ik, iw, za, qb, kb, vb, logf, zb, qc, zc, gates


def _index_scores(iq, iw, ik, q_pos, k_pos):
    s = jnp.einsum('bthd,bsd->bths', iq, ik, preferred_element_type=jnp.float32) * (D_IDX ** -0.5)
    score = jnp.einsum('bths,bth->bts', jax.nn.relu(s), iw.astype(jnp.float32) * (H_IDX ** -0.5))
    return jnp.where(k_pos[None, None, :] <= q_pos[None, :, None], score, -jnp.inf)


def _take_rows(a, idx):
    return jax.vmap(lambda rows, i: rows[i])(a, idx)


def _gather_paged(pool, page_table, new_rows, idx):
    b, t, k = idx.shape
    in_past = idx < PAST_LEN
    ip = jnp.minimum(idx, PAST_LEN - 1)
    page = jnp.take_along_axis(page_table, (ip // PAGE_SIZE).reshape(b, t * k), axis=1).reshape(b, t, k)
    flat = pool.reshape((-1,) + pool.shape[2:])
    g_past = flat[page * PAGE_SIZE + ip % PAGE_SIZE]
    g_new = _take_rows(new_rows, jnp.clip(idx - PAST_LEN, 0, t - 1))
    mask = in_past.reshape(in_past.shape + (1,) * (g_past.ndim - 3))
    return jnp.where(mask, g_past, g_new)


def _sparse_attend(q, k_sel, v_sel, valid):
    b, t = q.shape[:2]
    qg = q.reshape(b, t, KVH_A, H_A // KVH_A, HEAD_DIM)
    s = jnp.einsum('btngd,btknd->btngk', qg, k_sel, preferred_element_type=jnp.float32) * (HEAD_DIM ** -0.5)
    s = jnp.where(valid[:, :, None, None, :], s, -jnp.inf)
    p = jax.nn.softmax(s, axis=-1).astype(v_sel.dtype)
    o = jnp.einsum('btngk,btknd->btngd', p, v_sel)
    return o.reshape(b, t, W_A)


def _fox_attend(q, k, v, c_q, c_k, q_pos, k_pos):
    b, t = q.shape[:2]
    l = k.shape[1]
    grp = H_B // KVH_B
    qg = q.reshape(b, t, KVH_B, grp, HEAD_DIM)
    s = jnp.einsum('btngd,bsnd->bngts', qg, k, preferred_element_type=jnp.float32) * (HEAD_DIM ** -0.5)
    cq = c_q.reshape(b, t, KVH_B, grp).transpose(0, 2, 3, 1)[..., None]
    ck = c_k.reshape(b, l, KVH_B, grp).transpose(0, 2, 3, 1)[..., None, :]
    s = jnp.where(k_pos[None, :] <= q_pos[:, None], s + cq - ck, -jnp.inf)
    p = jax.nn.softmax(s, axis=-1).astype(v.dtype)
    o = jnp.einsum('bngts,bsnd->btngd', p, v)
    return o.reshape(b, t, W_B)


def _cross_attend(q, mk, mv):
    b, t = q.shape[:2]
    s = jnp.einsum('bthd,bmhd->bhtm', q, mk, preferred_element_type=jnp.float32) * (HD_C ** -0.5)
    p = jax.nn.softmax(s, axis=-1).astype(mv.dtype)
    return jnp.einsum('bhtm,bmhd->bthd', p, mv).reshape(b, t, W_C)


def _mem_kv(mem, g_mem, w_mem_kv):
    b, m, _ = mem.shape
    mk, mv = jnp.split(_rmsnorm(mem, g_mem) @ w_mem_kv, 2, axis=-1)
    return mk.reshape(b, m, H_C, HD_C), mv.reshape(b, m, H_C, HD_C)


def _merge(x, o_a, za, o_b, zb, o_c, zc, gates, w_br_a, w_br_b, w_br_c, w_out):
    g_a, g_b, g_c = jnp.split(gates, 3, axis=-1)
    h = (jax.nn.sigmoid(g_a) * ((o_a * jax.nn.silu(za)) @ w_br_a)
         + jax.nn.sigmoid(g_b) * ((o_b * jax.nn.silu(zb)) @ w_br_b)
         + jax.nn.sigmoid(g_c) * ((o_c * jax.nn.silu(zc)) @ w_br_c))
    return x + h @ w_out


def _prompt_layer(x, mem, g_mem, w_mem_kv, g_norm, w_in, b_forget, w_br_a, w_br_b, w_br_c, w_out):
    b, s, _ = x.shape
    pos = jnp.arange(s, dtype=jnp.int32)
    (qa, ka, va, iq, ik, iw, za, qb, kb, vb, logf, zb, qc, zc, gates) = _project(x, pos, g_norm, w_in, b_forget)
    topk = min(TOPK_MAX, s // 4)
    nb = s // Q_BLOCK

    def blk(a):
        return a.reshape((b, nb, Q_BLOCK) + a.shape[2:]).swapaxes(0, 1)

    def unblk(a):
        return a.swapaxes(0, 1).reshape((b, s) + a.shape[3:])

    pos_b = pos.reshape(nb, Q_BLOCK)

    def dsa_block(args):
        q_t, iq_t, iw_t, pos_t = args
        scores = _index_scores(iq_t, iw_t, ik, pos_t, pos)
        _, idx = lax.top_k(scores, topk)
        return _sparse_attend(q_t, _take_rows(ka, idx), _take_rows(va, idx), idx <= pos_t[None, :, None])

    o_a = unblk(lax.map(dsa_block, (blk(qa), blk(iq), blk(iw), pos_b)))

    c = jnp.cumsum(logf.astype(jnp.float32), axis=1)

    def fox_block(args):
        q_t, cq_t, pos_t = args
        return _fox_attend(q_t, kb, vb, cq_t, c, pos_t, pos)

    o_b = unblk(lax.map(fox_block, (blk(qb), blk(c), pos_b)))

    mk, mv = _mem_kv(mem, g_mem, w_mem_kv)
    o_c = _cross_attend(qc, mk, mv)
    y = _merge(x, o_a, za, o_b, zb, o_c, zc, gates, w_br_a, w_br_b, w_br_c, w_out)
    return y, (ka, va, ik, kb, vb, logf, mk, mv)


def _sample_layer(x, ck_a, cv_a, c_idx, ck_b, cv_b, c_logf, cm_k, cm_v, page_table,
                  g_norm, w_in, b_forget, w_br_a, w_br_b, w_br_c, w_out):
    b, t, _ = x.shape
    length = PAST_LEN + t
    pos = PAST_LEN + jnp.arange(t, dtype=jnp.int32)
    k_pos = jnp.arange(length, dtype=jnp.int32)
    (qa, ka, va, iq, ik, iw, za, qb, kb, vb, logf, zb, qc, zc, gates) = _project(x, pos, g_norm, w_in, b_forget)

    def past_rows(pool):
        rows = pool[page_table]
        return rows.reshape((b, PAST_LEN) + pool.shape[2:])

    ik_all = jnp.concatenate([past_rows(c_idx), ik], axis=1)
    scores = _index_scores(iq, iw, ik_all, pos, k_pos)
    _, idx = lax.top_k(scores, min(TOPK_MAX, length // 4))
    o_a = _sparse_attend(qa, _gather_paged(ck_a, page_table, ka, idx),
                         _gather_paged(cv_a, page_table, va, idx), idx <= pos[None, :, None])

    kb_all = jnp.concatenate([past_rows(ck_b), kb], axis=1)
    vb_all = jnp.concatenate([past_rows(cv_b), vb], axis=1)
    logf_all = jnp.concatenate([past_rows(c_logf), logf], axis=1)
    c = jnp.cumsum(logf_all.astype(jnp.float32), axis=1)
    o_b = _fox_attend(qb, kb_all, vb_all, c[:, PAST_LEN:], c, pos, k_pos)

    o_c = _cross_attend(qc, cm_k, cm_v)
    y = _merge(x, o_a, za, o_b, zb, o_c, zc, gates, w_br_a, w_br_b, w_br_c, w_out)
    return y, (ka, va, ik, kb, vb, logf)


def setup_inputs(seed: int = 0) -> dict:
    key = jax.random.key(seed)
    ks = jax.random.split(key, 24)
    f32 = jnp.float32
    n_pages = PAST_LEN // PAGE_SIZE
    n_used = DEC_BATCH * n_pages
    n_pool = n_used + max(1, n_used // 4)

    def nrm(k, shape, scale=1.0):
        return scale * jax.random.normal(k, shape, f32)

    perm = jax.random.permutation(ks[10], n_pool)
    page_table = perm[:n_used].reshape(DEC_BATCH, n_pages).astype(jnp.int32)
    return {
        'x_prompt': nrm(ks[0], (BATCH, SEQ, D_MODEL)),
        'x_sample': nrm(ks[1], (DEC_BATCH, DEC_SEQ, D_MODEL)),
        'cache_a_k': nrm(ks[2], (DEPTH, n_pool, PAGE_SIZE, KVH_A, HEAD_DIM)),
        'cache_a_v': nrm(ks[3], (DEPTH, n_pool, PAGE_SIZE, KVH_A, HEAD_DIM)),
        'cache_a_idx': nrm(ks[4], (DEPTH, n_pool, PAGE_SIZE, D_IDX)),
        'cache_b_k': nrm(ks[5], (DEPTH, n_pool, PAGE_SIZE, KVH_B, HEAD_DIM)),
        'cache_b_v': nrm(ks[6], (DEPTH, n_pool, PAGE_SIZE, KVH_B, HEAD_DIM)),
        'cache_b_logf': jax.nn.log_sigmoid(2.0 + nrm(ks[7], (DEPTH, n_pool, PAGE_SIZE, H_B))),
        'cache_mem_k': nrm(ks[8], (DEPTH, DEC_BATCH, N_MEM, H_C, HD_C)),
        'cache_mem_v': nrm(ks[9], (DEPTH, DEC_BATCH, N_MEM, H_C, HD_C)),
        'page_table': page_table,
        'mem_prompt': nrm(ks[11], (BATCH, N_MEM, D_MODEL)),
        'g_norm': 1.0 + nrm(ks[12], (DEPTH, D_MODEL), 0.02),
        'w_in': nrm(ks[13], (DEPTH, D_MODEL, D_IN), D_MODEL ** -0.5),
        'b_forget': 2.0 + nrm(ks[14], (DEPTH, H_B), 0.5),
        'w_br_a': nrm(ks[15], (DEPTH, W_A, D_MODEL), W_A ** -0.5),
        'w_br_b': nrm(ks[16], (DEPTH, W_B, D_MODEL), W_B ** -0.5),
        'w_br_c': nrm(ks[17], (DEPTH, W_C, D_MODEL), W_C ** -0.5),
        'w_out': nrm(ks[18], (DEPTH, D_MODEL, D_MODEL), D_MODEL ** -0.5),
        'g_mem': 1.0 + nrm(ks[19], (DEPTH, D_MODEL), 0.02),
        'w_mem_kv': nrm(ks[20], (DEPTH, D_MODEL, 2 * W_C), D_MODEL ** -0.5),
        'g_final': 1.0 + nrm(ks[21], (D_MODEL,), 0.02),
    }


def reference(x_prompt, x_sample, cache_a_k, cache_a_v, cache_a_idx, cache_b_k, cache_b_v, cache_b_logf,
              cache_mem_k, cache_mem_v, page_table, mem_prompt, g_norm, w_in, b_forget, w_br_a, w_br_b,
              w_br_c, w_out, g_mem, w_mem_kv, g_final):
    y_p, y_s = x_prompt, x_sample
    p_states, s_states = [], []
    for l in range(DEPTH):
        y_p, st_p = _prompt_layer(y_p, mem_prompt, g_mem[l], w_mem_kv[l], g_norm[l], w_in[l], b_forget[l],
                                  w_br_a[l], w_br_b[l], w_br_c[l], w_out[l])
        y_s, st_s = _sample_layer(y_s, cache_a_k[l], cache_a_v[l], cache_a_idx[l], cache_b_k[l], cache_b_v[l],
                                  cache_b_logf[l], cache_mem_k[l], cache_mem_v[l], page_table,
                                  g_norm[l], w_in[l], b_forget[l], w_br_a[l], w_br_b[l], w_br_c[l], w_out[l])
        p_states.append(st_p)
        s_states.append(st_s)
    p_ak, p_av, p_ai, p_bk, p_bv, p_bf, p_mk, p_mv = [jnp.stack(t) for t in zip(*p_states)]
    s_ak, s_av, s_ai, s_bk, s_bv, s_bf = [jnp.stack(t) for t in zip(*s_states)]
    y_prompt = _rmsnorm(y_p, g_final)
    y_sample = _rmsnorm(y_s, g_final)
    return (y_prompt, y_sample, p_ak, p_av, p_ai, p_bk, p_bv, p_bf, p_mk, p_mv,
            s_ak, s_av, s_ai, s_bk, s_bv, s_bf)
```

```python
import numpy as np
import concourse.bass as bass
import concourse.mybir as mybir
from concourse.bass_utils import run_bass_kernel_spmd

F32 = mybir.dt.float32
BF16 = mybir.dt.bfloat16
AF = mybir.ActivationFunctionType
ALU = mybir.AluOpType

D_MODEL = 4096
KC = D_MODEL // 128
BATCH, SEQ = 4, 2048
DEC_BATCH = 32
PAST_LEN = 8192
N_MEM = 256
EPS = 1e-6
ROPE_THETA = 500000.0
NCORES = 8
TOK_PER_CORE = BATCH * SEQ // NCORES
NPB = TOK_PER_CORE // 128
NBLK = NPB + 1
SAMP_PER_CORE = DEC_BATCH // NCORES
MEM_PER_CORE = BATCH * N_MEM // NCORES

WG = 128
KV_COLS = 512 * 4 + 64 + 12
C_KA, C_VA, C_KB, C_VB, C_IK, C_FB = 0, 512, 1024, 1536, 2048, 2112
IN_KA, IN_VA, IN_IK, IN_KB, IN_VB, IN_FB = 1536, 2048, 3584, 6736, 7248, 7760


class Prog:
    CH = 12000
    NDMA = 24

    def __init__(self, nc, stack):
        self.nc = nc
        self.stack = stack
        self.names = ['pe', 'act', 'dve', 'pool', 'sp']
        self.lists = {e: [] for e in self.names}
        self.count = {e: 0 for e in self.names}
        self.sems = {e: [] for e in self.names}
        self.seen = {e: {f: 0 for f in self.names} for e in self.names}
        self.dma_sems = [stack.enter_context(nc.semaphore("dq%d" % i)) for i in range(self.NDMA)]
        self.dma_uses = [0] * self.NDMA
        self.dma_seen = {e: [0] * self.NDMA for e in self.names}
        self.dma_next = 0
        self.lastw = {}
        self.readers = {}
        self.out_tokens = []

    def _sem(self, e, n):
        k = (n - 1) // self.CH
        while len(self.sems[e]) <= k:
            self.sems[e].append(self.stack.enter_context(self.nc.semaphore("c_%s_%d" % (e, len(self.sems[e])))))
        return self.sems[e][k], (n - 1) % self.CH + 1

    def _wait(self, e, tok, waits):
        if tok[0] == 'c':
            _, f, n = tok
            if f == e and e == 'pe':
                return
            if self.seen[e][f] >= n:
                return
            self.seen[e][f] = n
            sem, v = self._sem(f, n)
            waits.append((sem, v))
        else:
            _, idx, v = tok
            if self.dma_seen[e][idx] >= v:
                return
            self.dma_seen[e][idx] = v
            waits.append((self.dma_sems[idx], v))

    def _deps(self, e, reads, writes):
        waits = []
        deps = []
        for k in reads:
            if k in self.lastw:
                deps.append(self.lastw[k])
        for k in writes:
            if k in self.lastw:
                deps.append(self.lastw[k])
            deps.extend(self.readers.get(k, {}).values())
        for d in deps:
            self._wait(e, d, waits)
        return waits

    def _record(self, tok, reads, writes, ekey):
        for k in reads:
            self.readers.setdefault(k, {})[ekey] = tok
        for k in writes:
            self.lastw[k] = tok
            self.readers[k] = {}

    def op(self, e, fn, reads=(), writes=()):
        waits = self._deps(e, reads, writes)
        self.count[e] += 1
        n = self.count[e]
        sem, v = self._sem(e, n)
        self.lists[e].append((waits, fn, sem, 1))
        tok = ('c', e, n)
        self.seen[e][e] = max(self.seen[e][e], 0)
        self._record(tok, reads, writes, e)
        return tok

    def dma(self, q, out_ap, in_ap, reads=(), writes=(), is_output=False):
        idx = self.dma_next
        self.dma_next = (idx + 1) % self.NDMA
        waits = []
        if self.dma_uses[idx] > 0:
            self._wait(q, ('d', idx, 16 * self.dma_uses[idx]), waits)
        waits += self._deps(q, reads, writes)
        self.dma_uses[idx] += 1
        tok = ('d', idx, 16 * self.dma_uses[idx])
        fn = lambda eng, o=out_ap, i=in_ap: eng.dma_start(out=o, in_=i)
        self.lists[q].append((waits, fn, self.dma_sems[idx], 16))
        self._record(tok, reads, writes, ('dma', idx, self.dma_uses[idx]))
        if is_output:
            self.out_tokens.append(tok)
        return tok

    def barrier(self):
        for e in self.names:
            waits = []
            for f in self.names:
                if f != e and self.count[f] > 0:
                    self._wait(e, ('c', f, self.count[f]), waits)
            for idx in range(self.NDMA):
                if self.dma_uses[idx] > 0:
                    self._wait(e, ('d', idx, 16 * self.dma_uses[idx]), waits)
            self.lists[e].append((waits, None, None, 0))
        self.lastw = {}
        self.readers = {}

    def idma(self, out_ap, in_ap, idx_ap, reads=(), writes=()):
        q = 'pool'
        idx = self.dma_next
        self.dma_next = (idx + 1) % self.NDMA
        waits = []
        if self.dma_uses[idx] > 0:
            self._wait(q, ('d', idx, 16 * self.dma_uses[idx]), waits)
        waits += self._deps(q, reads, writes)
        self.dma_uses[idx] += 1
        tok = ('d', idx, 16 * self.dma_uses[idx])
        fn = lambda eng, o=out_ap, i=in_ap, x=idx_ap: eng.indirect_dma_start(
            out=o, out_offset=None, in_=i, in_offset=bass.IndirectOffsetOnAxis(x, 0))
        self.lists[q].append((waits, fn, self.dma_sems[idx], 16))
        self._record(tok, reads, writes, ('dma', idx, self.dma_uses[idx]))
        return tok

    def finish(self):
        waits = []
        for tok in self.out_tokens:
            self._wait('sp', tok, waits)
        self.lists['sp'].append((waits, None, None, 0))

    def emit(self):
        nc = self.nc

        def run(lst, eng):
            for waits, fn, sem, inc in lst:
                for s, v in waits:
                    eng.wait_ge(s, v)
                if fn is not None:
                    fn(eng).then_inc(sem, inc)

        with nc.Block() as block:
            @block.tensor
            def _(e):
                run(self.lists['pe'], e)

            @block.scalar
            def _(e):
                run(self.lists['act'], e)

            @block.vector
            def _(e):
                run(self.lists['dve'], e)

            @block.gpsimd
            def _(e):
                run(self.lists['pool'], e)

            @block.sync
            def _(e):
                run(self.lists['sp'], e)


DBG = {}

T = 256
NKB = SEQ // 128
NQB = 1024 // 128
SEGS = dict(qa=(0, 1536), ka=(1536, 512), va=(2048, 512), iq=(2560, 1024), ik=(3584, 64), iw=(3648, 16),
            za=(3664, 1536), qb=(5200, 1536), kb=(6736, 512), vb=(7248, 512), fb=(7760, 12), zb=(7772, 1536),
            qc=(9308, 1024), zc=(10332, 1024), ga=(11356, 4096), gb=(15452, 4096), gc=(19548, 4096))
D_IN = 23644
NEG = -1.0e30
SM_SCALE = 128 ** -0.5
SM_SCALE_C = 256 ** -0.5
TOPK = 256
N_BISECT = 22
N_POOL = 2560
N_BISECT_S = 30


def OP(P, eng, method, reads, writes, *args, **kw):
    P.op(eng, lambda e: getattr(e, method)(*args, **kw), reads=reads, writes=writes)


def build_nc():
    from contextlib import ExitStack
    nc = bass.Bass("TRN2", target_bir_lowering=False)
    di = lambda name, shape: nc.dram_tensor(name, shape, F32, kind="ExternalInput").ap()
    do = lambda name, shape: nc.dram_tensor(name, shape, F32, kind="ExternalOutput").ap()
    xk = di("xk", [SEQ, D_MODEL])
    xq = di("xq", [1024, D_MODEL])
    xsm = di("xsm", [128, D_MODEL])
    xmem = di("xmem", [N_MEM, D_MODEL])
    w_in = di("w_in", [D_MODEL, D_IN])
    w_bra = di("w_bra", [1536, D_MODEL])
    w_brb = di("w_brb", [1536, D_MODEL])
    w_brc = di("w_brc", [1024, D_MODEL])
    w_o = di("w_o", [D_MODEL, D_MODEL])
    w_mem = di("w_mem", [D_MODEL, 2048])
    gcols = di("gcols", [128, 2 * KC])
    gfin = di("gfin", [128, D_MODEL])
    bfrow = di("bfrow", [128, 12])
    rotk = di("rotk", [128, (NKB + 1) * 48])
    rotq = di("rotq", [128, NQB * 48])
    cst = di("cst", [128, 512])
    tabs = di("tabs", [128, 8 + 16 + 8])
    qrow_d = di("qrow", [128, 1024])
    kposrow_d = di("kposrow", [128, SEQ])
    selbc_d = di("selbc", [128, NQB * NKB * 12])
    NPOOL_ROWS = N_POOL * 128
    cak = di("cak", [NPOOL_ROWS, 512])
    cav = di("cav", [NPOOL_ROWS, 512])
    cbk = di("cbk", [NPOOL_ROWS, 512])
    cbv = di("cbv", [NPOOL_ROWS, 512])
    cai = di("cai", [NPOOL_ROWS, 64])
    cbl = di("cbl", [NPOOL_ROWS, 12])
    cmk = di("cmk", [SAMP_PER_CORE * N_MEM, 1024])
    cmv = di("cmv", [SAMP_PER_CORE * N_MEM, 1024])
    ptb_d = nc.dram_tensor("ptb", [128, SAMP_PER_CORE * 64], mybir.dt.int32, kind="ExternalInput").ap()
    smc_d = di("smc", [128, 4 * 128 + 128 + 8])
    oys = do("oys", [128, D_MODEL])
    okv = do("okv", [SEQ, KV_COLS])
    osamp = do("osamp", [128, KV_COLS])
    omem = do("omem", [N_MEM, 2048])
    oy = do("oy", [1024, D_MODEL])
    odbg = do("odbg", [4096, T]) if DBG.get('dump') else None

    with ExitStack() as st:
        P = Prog(nc, st)
        sb = lambda name, shape, d=F32: st.enter_context(nc.sbuf_tensor(name, shape, d))
        ps = lambda name, shape, d=F32: st.enter_context(nc.psum_tensor(name, shape, d))

        kaT = sb("kaT", [128, 4, SEQ], BF16)
        kbT = sb("kbT", [128, 4, SEQ], BF16)
        va = sb("va", [128, NKB, 4, 130], BF16)
        vb = sb("vb", [128, NKB, 4, 130], BF16)
        ik2T = sb("ik2T", [128, SEQ], BF16)
        negc = sb("negc", [128, NKB, 12])
        cprev = sb("cprev", [128, 12])
        mkT = sb("mkT", [128, 8, N_MEM], BF16)
        mv = sb("mv", [128, 2, 4, 258], BF16)
        cs = sb("cs", [128, 512])
        ident_b = sb("ident_b", [128, 128], BF16)
        gc = sb("gc", [128, 2 * KC])
        bfr = sb("bfr", [128, 12])
        rk = sb("rk", [128, (NKB + 1) * 48])
        rq = sb("rq", [128, NQB * 48])
        tb_ = sb("tabs_sb", [128, 32])
        U_, E127_, E64_ = cs[:, 128:256], cs[:, 256:384], cs[:, 384:512]
        kq_, kpos_, qpos_ = tb_[:, 0:8], tb_[:, 8:24], tb_[:, 24:32]
        xs = sb("xs", [128, D_MODEL])
        xnT = sb("xnT", [128, KC, T], BF16)
        wst = [sb("wst%d" % i, [128, 8, WG]) for i in range(3)]
        wbf = [sb("wbf%d" % i, [128, KC, WG], BF16) for i in range(2)]
        ost = [sb("ost%d" % i, [128, T]) for i in range(3)]
        ss = sb("ss", [128, 4])
        tmp = sb("tmp", [128, 8, 16])
        lf = sb("lf", [128, 3, 12])
        kbf = sb("kbf", [128, 128], BF16)
        ozT = sb("ozT", [128, 32, T], BF16)
        qT = sb("qT", [128, 12, T], BF16)
        iqT = sb("iqT", [128, 8, T], BF16)
        iw = sb("iw", [128, 2, 16])
        R16 = sb("R16", [128, 4096])
        I_ = R16[:, 0:2048]
        xb = R16[:, 0:2048].bitcast(BF16)
        msk = R16[:, 2048:4096]
        M_ = R16[:, 2048:3072].bitcast(BF16)
        MT = R16[:, 3072:4096].bitcast(BF16)
        hT = R16[:, :].bitcast(BF16)
        RKEYS = ['I', 'M', 'MT']
        PT = [sb("PT%d" % i, [128, 384], BF16) for i in range(2)]
        osb = sb("osb", [128, 392])
        rcp = sb("rcp", [128, 4])
        onb = sb("onb", [128, 384], BF16)
        bis = sb("bis", [128, 8])
        fbias = sb("fbias", [128, NKB, 12])
        tmpc = sb("tmpc", [128, NKB, 12])
        selb = sb("selb", [128, NKB * 12])
        cbn = sb("cbn", [128, 12])
        qrow = sb("qrow_sb", [128, T])
        zs = sb("zs", [128, T], BF16)
        sg = sb("sg", [128, T])
        brs = sb("brs", [128, 3, T])
        hacc = sb("hacc", [128, T])
        gpc = [sb("gpc%d" % i, [128, 512]) for i in range(2)]
        rl = gpc

        pt = [ps("pt%d" % i, [128, 1024], BF16) for i in range(2)]
        pm = [ps("pm%d" % i, [128, 512]) for i in range(4)]
        pa = [ps("pa%d" % i, [128, 512]) for i in range(2)]

        cnt = dict(pmS=0, pm=0, pt=0, pa=0, w=0, stg=0, ost=0, cast=0, PT=0, rl=0, gp=0)

        def nxt(k, n):
            v = cnt[k] % n
            cnt[k] += 1
            return v

        P.dma('sp', cs[:, :], cst, writes=['cs'])
        P.dma('sp', gc[:, :], gcols, writes=['gc'])
        P.dma('sp', bfr[:, :], bfrow, writes=['bfr'])
        P.dma('sp', rk[:, :], rotk, writes=['rk'])
        P.dma('sp', rq[:, :], rotq, writes=['rq'])
        P.dma('sp', tb_[:, :], tabs, writes=['tabs'])
        OP(P, 'dve', 'tensor_copy', ['cs'], ['ident_b'], out=ident_b[:, :], in_=cs[:, 0:128])
        OP(P, 'dve', 'memset', [], ['va'], va[:, :, :, 128:129], 1.0)
        OP(P, 'dve', 'memset', [], ['vb'], vb[:, :, :, 128:129], 1.0)
        OP(P, 'dve', 'memset', [], ['mv'], mv[:, :, :, 256:257], 1.0)
        OP(P, 'dve', 'memset', [], ['cprev'], cprev[:, :], 0.0)

        def norm_block(src_rows, col0, goff):
            P.dma('sp', xs[:, :], src_rows, writes=['xs'])
            OP(P, 'dve', 'memset', [], ['ss'], ss[:, :], 0.0)
            OP(P, 'act', 'activation', ['xs'], ['I', 'ss'], out=xb, in_=xs[:, :], func=AF.Square, accum_out=ss[:, 0:1])
            OP(P, 'dve', 'tensor_scalar', ['ss'], ['ss'], out=ss[:, 1:2], in0=ss[:, 0:1], scalar1=1.0 / D_MODEL,
               scalar2=EPS, op0=ALU.mult, op1=ALU.add)
            OP(P, 'act', 'activation', ['ss'], ['ss'], out=ss[:, 3:4], in_=ss[:, 1:2], func=AF.Sqrt)
            OP(P, 'dve', 'reciprocal', ['ss'], ['ss'], out=ss[:, 2:3], in_=ss[:, 3:4])
            OP(P, 'dve', 'tensor_scalar', ['xs', 'ss'], ['I'], out=xb, in0=xs[:, :], scalar1=ss[:, 2:3], scalar2=None,
               op0=ALU.mult)
            for g4 in range(4):
                pb = nxt('pt', 2)
                for j in range(8):
                    kc = g4 * 8 + j
                    OP(P, 'pe', 'transpose', ['I', 'ident_b'], ['pt%d' % pb], out=pt[pb][:, j * 128:(j + 1) * 128],
                       in_=xb[:, kc * 128:(kc + 1) * 128], identity=ident_b[:, :])
                for j in range(8):
                    kc = g4 * 8 + j
                    OP(P, 'dve', 'tensor_scalar', ['pt%d' % pb, 'gc'], ['xnT'], out=xnT[:, kc, col0:col0 + 128],
                       in0=pt[pb][:, j * 128:(j + 1) * 128], scalar1=gc[:, goff + kc:goff + kc + 1], scalar2=None,
                       op0=ALU.mult)

        def wload(parts, K):
            wi = nxt('w', 2)
            for (w2d, c0, ncol, dcol) in parts:
                wv = w2d.rearrange("(kc p) c -> p kc c", p=128)
                for k0 in range(0, K, 8):
                    kn = min(8, K - k0)
                    sg_ = nxt('stg', 3)
                    P.dma('sp', wst[sg_][:, 0:kn, 0:ncol], wv[:, k0:k0 + kn, c0:c0 + ncol], writes=['wst%d' % sg_])
                    eng = ('pool', 'act', 'pool', 'dve')[nxt('cast', 4)]
                    if eng == 'act':
                        OP(P, 'act', 'copy', ['wst%d' % sg_], ['wbf%d_%d' % (wi, k0 // 8)],
                           out=wbf[wi][:, k0:k0 + kn, dcol:dcol + ncol], in_=wst[sg_][:, 0:kn, 0:ncol])
                    else:
                        OP(P, eng, 'tensor_copy', ['wst%d' % sg_], ['wbf%d_%d' % (wi, k0 // 8)],
                           out=wbf[wi][:, k0:k0 + kn, dcol:dcol + ncol], in_=wst[sg_][:, 0:kn, 0:ncol])
            return wi

        def mm_tm(wi, tok0, ncol, K=KC):
            pb = nxt('pm', 4)
            for kc in range(K):
                OP(P, 'pe', 'matmul', ['xnT', 'wbf%d_%d' % (wi, kc // 8)], ['pm%d' % pb], pm[pb][:, 0:ncol],
                   lhsT=xnT[:, kc, tok0:tok0 + 128], rhs=wbf[wi][:, kc, 0:ncol], start=(kc == 0), stop=(kc == K - 1))
            return pb

        def mm_fm(wi, ncol, rhsT, rkey, K, koff=0, ntok=T, pool='pm'):
            if pool == 'pm':
                pb = nxt('pm', 4)
                dst, dkey = pm[pb], 'pm%d' % pb
            else:
                pb = nxt('pa', 2)
                dst, dkey = pa[pb], 'pa%d' % pb
            for kc in range(K):
                OP(P, 'pe', 'matmul', [rkey, 'wbf%d_%d' % (wi, kc // 8)], [dkey], dst[0:ncol, 0:ntok],
                   lhsT=wbf[wi][:, kc, 0:ncol], rhs=rhsT[:, koff + kc, 0:ntok], start=(kc == 0), stop=(kc == K - 1))
            return dst, dkey

        def evac_tm(pb, ncol):
            o = nxt('ost', 3)
            OP(P, 'act', 'copy', ['pm%d' % pb], ['ost%d' % o], out=ost[o][:, 0:ncol], in_=pm[pb][:, 0:ncol])
            return o

        def rot(o, off, half, cos, sin, rkey):
            okey = 'ost%d' % o
            x1 = ost[o][:, off:off + half]
            x2 = ost[o][:, off + half:off + 2 * half]
            t = lambda i: tmp[:, i, 0:half]
            OP(P, 'dve', 'tensor_tensor', [okey, rkey], ['tmp'], out=t(0), in0=x1, in1=cos, op=ALU.mult)
            OP(P, 'dve', 'tensor_tensor', [okey, rkey], ['tmp'], out=t(1), in0=x2, in1=sin, op=ALU.mult)
            OP(P, 'dve', 'tensor_tensor', [okey, rkey], ['tmp'], out=t(2), in0=x1, in1=sin, op=ALU.mult)
            OP(P, 'dve', 'tensor_tensor', [okey, rkey], ['tmp'], out=t(3), in0=x2, in1=cos, op=ALU.mult)
            OP(P, 'dve', 'tensor_tensor', ['tmp'], [okey], out=x1, in0=t(0), in1=t(1), op=ALU.subtract)
            OP(P, 'dve', 'tensor_tensor', ['tmp'], [okey], out=x2, in0=t(2), in1=t(3), op=ALU.add)

        def transpose_out(src_bf, skey, nrow_out, dst, dkey, eng='act'):
            c = cnt['pt']
            cnt['pt'] += 1
            bank, slot = (c // 8) % 2, c % 8
            OP(P, 'pe', 'transpose', [skey, 'ident_b'], ['pt%d' % bank], out=pt[bank][0:nrow_out, slot * 128:(slot + 1) * 128],
               in_=src_bf, identity=ident_b[:, :])
            if eng == 'act':
                OP(P, 'act', 'copy', ['pt%d' % bank], [dkey], out=dst, in_=pt[bank][0:nrow_out, slot * 128:(slot + 1) * 128])
            else:
                OP(P, 'dve', 'tensor_copy', ['pt%d' % bank], [dkey], out=dst, in_=pt[bank][0:nrow_out, slot * 128:(slot + 1) * 128])

        def logf_chain(o, fo):
            okey = 'ost%d' % o
            OP(P, 'dve', 'tensor_tensor', [okey, 'bfr'], ['lf'], out=lf[:, 0, :], in0=ost[o][:, fo:fo + 12], in1=bfr[:, :],
               op=ALU.add)
            OP(P, 'act', 'activation', ['lf'], ['lf'], out=lf[:, 1, :], in_=lf[:, 0, :], func=AF.Exp, scale=-1.0)
            OP(P, 'dve', 'tensor_scalar', ['lf'], ['lf'], out=lf[:, 2, :], in0=lf[:, 1, :], scalar1=1.0, scalar2=None,
               op0=ALU.add)
            OP(P, 'act', 'activation', ['lf'], ['lf'], out=lf[:, 1, :], in_=lf[:, 2, :], func=AF.Ln)
            OP(P, 'dve', 'tensor_scalar', ['lf'], [okey], out=ost[o][:, fo:fo + 12], in0=lf[:, 1, :], scalar1=-1.0,
               scalar2=None, op0=ALU.mult)

        kv_groups = ([('ka', h) for h in range(4)] + [('va', h) for h in range(4)] + [('kb', h) for h in range(4)]
                     + [('vb', h) for h in range(4)] + [('ikfb', 0)])
        OUTC = dict(ka=C_KA, va=C_VA, kb=C_KB, vb=C_VB, ikfb=C_IK)

        def kv_tile(rows_ap, nblk, kb0, out_ap, rtab, rtab_key, rtab_blk0, resident, smp=None):
            for (kind, h) in kv_groups:
                if kind == 'ikfb':
                    parts = [(w_in, SEGS['ik'][0], 64, 0), (w_in, SEGS['fb'][0], 12, 64)]
                    ncol = 76
                else:
                    parts = [(w_in, SEGS[kind][0] + h * 128, 128, 0)]
                    ncol = 128
                wi = wload(parts, KC)
                for b in range(nblk):
                    kb = kb0 + b
                    pb = mm_tm(wi, b * 128, ncol)
                    o = evac_tm(pb, ncol)
                    okey = 'ost%d' % o
                    rb = (rtab_blk0 + b) * 48
                    if kind == 'ka':
                        rot(o, 0, 16, rtab[:, rb:rb + 16], rtab[:, rb + 16:rb + 32], rtab_key)
                    if kind == 'ikfb':
                        rot(o, 0, 8, rtab[:, rb + 32:rb + 40], rtab[:, rb + 40:rb + 48], rtab_key)
                        logf_chain(o, 64)
                    c0 = OUTC[kind] + (h * 128 if kind != 'ikfb' else 0)
                    P.dma('act', out_ap[b * 128:(b + 1) * 128, c0:c0 + ncol], ost[o][:, 0:ncol], reads=[okey], is_output=True)
                    if smp is not None:
                        if kind in ('ka', 'kb'):
                            OP(P, 'dve', 'tensor_copy', [okey], ['kbf'], out=kbf[:, :], in_=ost[o][:, 0:128])
                            transpose_out(kbf[:, :], 'kbf', 128, smp[kind][:, h * 128:(h + 1) * 128], 's' + kind)
                        elif kind in ('va', 'vb'):
                            OP(P, 'dve', 'tensor_copy', [okey], ['s' + kind], out=smp[kind][:, h * 130:h * 130 + 128], in_=ost[o][:, 0:128])
                        else:
                            OP(P, 'dve', 'tensor_copy', [okey], ['kbf'], out=kbf[:, 0:64], in_=ost[o][:, 0:64])
                            OP(P, 'dve', 'tensor_copy', [okey], ['kbf'], out=kbf[:, 64:128], in_=ost[o][:, 0:64])
                            transpose_out(kbf[:, :], 'kbf', 128, smp['ik'][:, :], 'sik')
                            OP(P, 'dve', 'tensor_copy', [okey], ['slogf'], out=smp['logf'], in_=ost[o][:, 64:76])
                        continue
                    if not resident:
                        continue
                    if kind in ('ka', 'kb'):
                        OP(P, 'dve', 'tensor_copy', [okey], ['kbf'], out=kbf[:, :], in_=ost[o][:, 0:128])
                        dstT = kaT if kind == 'ka' else kbT
                        transpose_out(kbf[:, :], 'kbf', 128, dstT[:, h, kb * 128:(kb + 1) * 128], kind + 'T')
                    elif kind in ('va', 'vb'):
                        dv = va if kind == 'va' else vb
                        OP(P, 'dve', 'tensor_copy', [okey], [kind], out=dv[:, kb, h, 0:128], in_=ost[o][:, 0:128])
                    else:
                        OP(P, 'dve', 'tensor_copy', [okey], ['kbf'], out=kbf[:, 0:64], in_=ost[o][:, 0:64])
                        OP(P, 'dve', 'tensor_copy', [okey], ['kbf'], out=kbf[:, 64:128], in_=ost[o][:, 0:64])
                        transpose_out(kbf[:, :], 'kbf', 128, ik2T[:, kb * 128:(kb + 1) * 128], 'ik2T')
                        pb2 = nxt('pa', 2)
                        OP(P, 'pe', 'matmul', [okey, 'cs'], ['pa%d' % pb2], pa[pb2][:, 0:12], lhsT=U_, rhs=ost[o][:, 64:76],
                           start=True, stop=False)
                        OP(P, 'pe', 'matmul', ['cprev', 'cs'], ['pa%d' % pb2], pa[pb2][:, 0:12], lhsT=E127_, rhs=cprev[:, :],
                           start=False, stop=True)
                        OP(P, 'act', 'copy', ['pa%d' % pb2], ['cprev'], out=cprev[:, :], in_=pa[pb2][:, 0:12])
                        OP(P, 'dve', 'tensor_scalar', ['cprev'], ['negc'], out=negc[:, kb, :], in0=cprev[:, :], scalar1=-1.0,
                           scalar2=None, op0=ALU.mult)

        n_kt = DBG.get('n_kt', SEQ // T)
        for kt in range(n_kt):
            for b in range(2):
                norm_block(xk[(kt * 2 + b) * 128:(kt * 2 + b + 1) * 128, :], b * 128, 0)
            kv_tile(None, 2, kt * 2, okv[kt * T:(kt + 1) * T, :], rk, 'rk', kt * 2, True)
        if DBG.get('mem', True):
            for b in range(2):
                norm_block(xmem[b * 128:(b + 1) * 128, :], b * 128, KC)
            for g in range(16):
                wi = wload([(w_mem, g * 128, 128, 0)], KC)
                for b in range(2):
                    pb = mm_tm(wi, b * 128, 128)
                    o = evac_tm(pb, 128)
                    okey = 'ost%d' % o
                    P.dma('act', omem[b * 128:(b + 1) * 128, g * 128:(g + 1) * 128], ost[o][:, 0:128], reads=[okey],
                          is_output=True)
                    if g < 8:
                        OP(P, 'dve', 'tensor_copy', [okey], ['kbf'], out=kbf[:, :], in_=ost[o][:, 0:128])
                        transpose_out(kbf[:, :], 'kbf', 128, mkT[:, g, b * 128:(b + 1) * 128], 'mkT')
                    else:
                        hh, dc = (g - 8) // 2, (g - 8) % 2
                        OP(P, 'dve', 'tensor_copy', [okey], ['mv'], out=mv[:, b, hh, dc * 128:(dc + 1) * 128],
                           in_=ost[o][:, 0:128])

        def softmax_out(accs, nh, dhead, ch0, blk):
            stride = dhead + 2
            for g in range(nh):
                at, ak = accs[g]
                OP(P, 'act', 'copy', [ak], ['osb'], out=osb[:, g * stride:g * stride + dhead + 1], in_=at[:, 0:dhead + 1])
            for g in range(nh):
                OP(P, 'dve', 'reciprocal', ['osb'], ['rcp'], out=rcp[:, g:g + 1],
                   in_=osb[:, g * stride + dhead:g * stride + dhead + 1])
                OP(P, 'dve', 'tensor_scalar', ['osb', 'rcp'], ['onb'], out=onb[:, g * dhead:(g + 1) * dhead],
                   in0=osb[:, g * stride:g * stride + dhead], scalar1=rcp[:, g:g + 1], scalar2=None, op0=ALU.mult)
            for c in range(nh * dhead // 128):
                transpose_out(onb[:, c * 128:(c + 1) * 128], 'onb', 128, ozT[:, ch0 + c, blk * 128:(blk + 1) * 128], 'ozT')

        def attend_ab(b, kT, kkey, vv, vkey, ch0, fb):
            accs = [(pa[0], 'pa0'), (pa[1], 'pa1'), (pm[3], 'pm3')]
            for n in range(4):
                for i in range(NKB):
                    pb = nxt('pmS', 3)
                    for g in range(3):
                        OP(P, 'pe', 'matmul', [kkey, 'qT'], ['pm%d' % pb], pm[pb][:, g * 128:(g + 1) * 128],
                           lhsT=kT[:, n, i * 128:(i + 1) * 128], rhs=qT[:, 3 * n + g, b * 128:(b + 1) * 128], start=True, stop=True)
                    y = nxt('PT', 2)
                    if fb is None:
                        OP(P, 'act', 'activation', ['pm%d' % pb], ['PT%d' % y], out=PT[y][:, 0:384], in_=pm[pb][:, 0:384],
                           func=AF.Exp, scale=SM_SCALE)
                    else:
                        for g in range(3):
                            OP(P, 'act', 'activation', ['pm%d' % pb, 'fbias'], ['PT%d' % y], out=PT[y][:, g * 128:(g + 1) * 128],
                               in_=pm[pb][:, g * 128:(g + 1) * 128], func=AF.Exp, scale=SM_SCALE,
                               bias=fb[:, i, 3 * n + g:3 * n + g + 1])
                    for g in range(3):
                        OP(P, 'pool', 'tensor_tensor', ['PT%d' % y, 'MT'], ['PT%d' % y], out=PT[y][:, g * 128:(g + 1) * 128],
                           in0=PT[y][:, g * 128:(g + 1) * 128], in1=MT[:, i * 128:(i + 1) * 128], op=ALU.mult)
                    for g in range(3):
                        at, ak = accs[g]
                        OP(P, 'pe', 'matmul', ['PT%d' % y, vkey], [ak], at[:, 0:129],
                           lhsT=PT[y][:, g * 128:(g + 1) * 128], rhs=vv[:, i, n, 0:129], start=(i == 0), stop=(i == NKB - 1))
                softmax_out(accs, 3, 128, ch0 + 3 * n, b)

        def gate_z(seg, ch0, nch):
            for c in range(nch):
                wi = wload([(w_in, SEGS[seg][0] + c * 128, 128, 0)], KC)
                dst, dkey = mm_fm(wi, 128, xnT, 'xnT', KC)
                OP(P, 'act', 'activation', [dkey], ['zs'], out=zs[:, :], in_=dst[:, 0:T], func=AF.Silu)
                OP(P, 'pool', 'tensor_tensor', ['zs', 'ozT'], ['ozT'], out=ozT[:, ch0 + c, :], in0=ozT[:, ch0 + c, :],
                   in1=zs[:, :], op=ALU.mult)

        def q_fm(seg, nch):
            for c in range(nch):
                wi = wload([(w_in, SEGS[seg][0] + c * 128, 128, 0)], KC)
                dst, dkey = mm_fm(wi, 128, xnT, 'xnT', KC)
                OP(P, 'act', 'copy', [dkey], ['qT'], out=qT[:, c, :], in_=dst[:, 0:T])

        def phase_D():
            for fc in range(32):
                wa = wload([(w_bra, fc * 128, 128, 0)], 12)
                dA, kA = mm_fm(wa, 128, ozT, 'ozT', 12, koff=0)
                OP(P, 'act', 'copy', [kA], ['brs'], out=brs[:, 0, :], in_=dA[:, 0:T])
                wb_ = wload([(w_brb, fc * 128, 128, 0)], 12)
                dB, kB = mm_fm(wb_, 128, ozT, 'ozT', 12, koff=12)
                OP(P, 'act', 'copy', [kB], ['brs'], out=brs[:, 1, :], in_=dB[:, 0:T])
                wc_ = wload([(w_brc, fc * 128, 128, 0)], 8)
                dC, kC = mm_fm(wc_, 128, ozT, 'ozT', 8, koff=24)
                OP(P, 'act', 'copy', [kC], ['brs'], out=brs[:, 2, :], in_=dC[:, 0:T])
                for bi, seg in enumerate(('ga', 'gb', 'gc')):
                    wg_ = wload([(w_in, SEGS[seg][0] + fc * 128, 128, 0)], KC)
                    dG, kG = mm_fm(wg_, 128, xnT, 'xnT', KC)
                    OP(P, 'act', 'activation', [kG], ['sg'], out=sg[:, :], in_=dG[:, 0:T], func=AF.Sigmoid)
                    if bi == 0:
                        OP(P, 'dve', 'tensor_tensor', ['sg', 'brs'], ['hacc'], out=hacc[:, :], in0=sg[:, :], in1=brs[:, 0, :],
                           op=ALU.mult)
                    else:
                        OP(P, 'dve', 'tensor_tensor', ['sg', 'brs'], ['sg'], out=sg[:, :], in0=sg[:, :], in1=brs[:, bi, :],
                           op=ALU.mult)
                        OP(P, 'dve', 'tensor_tensor', ['sg', 'hacc'], ['hacc'], out=hacc[:, :], in0=hacc[:, :], in1=sg[:, :],
                           op=ALU.add)
                OP(P, 'dve', 'tensor_copy', ['hacc'], RKEYS, out=hT[:, fc * T:(fc + 1) * T], in_=hacc[:, :])

        def phase_E(b, x_rows, y_rows):
            P.dma('sp', xs[:, :], x_rows, writes=['xs'])
            for cg in range(32):
                wo = wload([(w_o, cg * 128, 128, 0)], KC)
                pb = nxt('pm', 4)
                for kc in range(KC):
                    OP(P, 'pe', 'matmul', RKEYS + ['wbf%d_%d' % (wo, kc // 8)], ['pm%d' % pb], pm[pb][:, 0:128],
                       lhsT=hT[:, kc * T + b * 128:kc * T + (b + 1) * 128], rhs=wbf[wo][:, kc, 0:128],
                       start=(kc == 0), stop=(kc == KC - 1))
                o = evac_tm(pb, 128)
                OP(P, 'dve', 'tensor_tensor', ['ost%d' % o, 'xs'], ['xs'], out=xs[:, cg * 128:(cg + 1) * 128],
                   in0=xs[:, cg * 128:(cg + 1) * 128], in1=ost[o][:, 0:128], op=ALU.add)
            OP(P, 'dve', 'memset', [], ['ss'], ss[:, :], 0.0)
            for pc in range(8):
                gi = nxt('gp', 2)
                OP(P, 'dve', 'tensor_tensor', ['xs'], ['gpc%d' % gi], out=gpc[gi][:, :], in0=xs[:, pc * 512:(pc + 1) * 512],
                   in1=xs[:, pc * 512:(pc + 1) * 512], op=ALU.mult)
                OP(P, 'dve', 'tensor_reduce', ['gpc%d' % gi], ['rcp'], out=rcp[:, 0:1], in_=gpc[gi][:, :],
                   axis=mybir.AxisListType.X, op=ALU.add)
                OP(P, 'dve', 'tensor_tensor', ['rcp', 'ss'], ['ss'], out=ss[:, 0:1], in0=ss[:, 0:1], in1=rcp[:, 0:1],
                   op=ALU.add)
            OP(P, 'dve', 'tensor_scalar', ['ss'], ['ss'], out=ss[:, 1:2], in0=ss[:, 0:1], scalar1=1.0 / D_MODEL,
               scalar2=EPS, op0=ALU.mult, op1=ALU.add)
            OP(P, 'act', 'activation', ['ss'], ['ss'], out=ss[:, 3:4], in_=ss[:, 1:2], func=AF.Sqrt)
            OP(P, 'dve', 'reciprocal', ['ss'], ['ss'], out=ss[:, 2:3], in_=ss[:, 3:4])
            OP(P, 'dve', 'tensor_scalar', ['xs', 'ss'], ['xs'], out=xs[:, :], in0=xs[:, :], scalar1=ss[:, 2:3],
               scalar2=None, op0=ALU.mult)
            for pc in range(8):
                gi = nxt('gp', 2)
                P.dma('sp', gpc[gi][:, :], gfin[:, pc * 512:(pc + 1) * 512], writes=['gpc%d' % gi])
                OP(P, 'dve', 'tensor_tensor', ['gpc%d' % gi, 'xs'], ['xs'], out=xs[:, pc * 512:(pc + 1) * 512],
                   in0=xs[:, pc * 512:(pc + 1) * 512], in1=gpc[gi][:, :], op=ALU.mult)
            P.dma('act', y_rows, xs[:, :], reads=['xs'], is_output=True)


        n_qt = DBG.get('n_qt', 1024 // T)
        phases = DBG.get('phases', 'ABCDE')
        for qt in range(n_qt):
            for b in range(2):
                norm_block(xq[(qt * 2 + b) * 128:(qt * 2 + b + 1) * 128, :], b * 128, 0)
            P.dma('sp', qrow[:, :], qrow_d[:, qt * T:(qt + 1) * T], writes=['qrow'])
            if 'A' in phases:
                for h in range(12):
                    wi = wload([(w_in, SEGS['qa'][0] + h * 128, 128, 0)], KC)
                    for b in range(2):
                        jb = qt * 2 + b
                        pb = mm_tm(wi, b * 128, 128)
                        o = evac_tm(pb, 128)
                        rot(o, 0, 16, rq[:, jb * 48:jb * 48 + 16], rq[:, jb * 48 + 16:jb * 48 + 32], 'rq')
                        OP(P, 'dve', 'tensor_copy', ['ost%d' % o], ['kbf'], out=kbf[:, :], in_=ost[o][:, 0:128])
                        transpose_out(kbf[:, :], 'kbf', 128, qT[:, h, b * 128:(b + 1) * 128], 'qT')
                for g in range(8):
                    wi = wload([(w_in, SEGS['iq'][0] + g * 128, 128, 0)], KC)
                    for b in range(2):
                        jb = qt * 2 + b
                        pb = mm_tm(wi, b * 128, 128)
                        o = evac_tm(pb, 128)
                        for hh in range(2):
                            rot(o, hh * 64, 8, rq[:, jb * 48 + 32:jb * 48 + 40], rq[:, jb * 48 + 40:jb * 48 + 48], 'rq')
                        OP(P, 'dve', 'tensor_copy', ['ost%d' % o], ['kbf'], out=kbf[:, :], in_=ost[o][:, 0:128])
                        transpose_out(kbf[:, :], 'kbf', 128, iqT[:, g, b * 128:(b + 1) * 128], 'iqT')
                wi = wload([(w_in, SEGS['iw'][0], 16, 0)], KC)
                for b in range(2):
                    pb = mm_tm(wi, b * 128, 16)
                    OP(P, 'act', 'activation', ['pm%d' % pb], ['iw'], out=iw[:, b, :], in_=pm[pb][:, 0:16], func=AF.Copy,
                       scale=(64 ** -0.5) * (16 ** -0.5))
                for b in range(2):
                    jb = qt * 2 + b
                    OP(P, 'dve', 'memset', [], ['I'], I_, 0.0)
                    for h in range(16):
                        g, hh = h // 2, h % 2
                        for q in range(4):
                            pb = nxt('pm', 4)
                            OP(P, 'pe', 'matmul', ['iqT', 'ik2T'], ['pm%d' % pb], pm[pb][:, 0:512],
                               lhsT=iqT[hh * 64:(hh + 1) * 64, g, b * 128:(b + 1) * 128],
                               rhs=ik2T[hh * 64:(hh + 1) * 64, q * 512:(q + 1) * 512], start=True, stop=True)
                            r = nxt('rl', 2)
                            OP(P, 'act', 'activation', ['pm%d' % pb], ['gpc%d' % r], out=rl[r][:, :], in_=pm[pb][:, 0:512],
                               func=AF.Relu)
                            OP(P, 'dve', 'scalar_tensor_tensor', ['gpc%d' % r, 'iw', 'I'], ['I'], out=I_[:, q * 512:(q + 1) * 512],
                               in0=rl[r][:, :], scalar=iw[:, b, h:h + 1], in1=I_[:, q * 512:(q + 1) * 512], op0=ALU.mult,
                               op1=ALU.add)
                    OP(P, 'dve', 'tensor_reduce', ['I'], ['bis'], out=bis[:, 0:1], in_=I_, axis=mybir.AxisListType.X, op=ALU.min)
                    OP(P, 'dve', 'tensor_reduce', ['I'], ['bis'], out=bis[:, 5:6], in_=I_, axis=mybir.AxisListType.X, op=ALU.max)
                    P.dma('sp', msk, kposrow_d, writes=['M', 'MT'])
                    OP(P, 'dve', 'tensor_scalar', ['M', 'MT', 'tabs'], ['M', 'MT'], out=msk, in0=msk, scalar1=qpos_[:, jb:jb + 1],
                       scalar2=None, op0=ALU.is_gt)
                    OP(P, 'dve', 'scalar_tensor_tensor', ['M', 'MT', 'I'], ['I'], out=I_, in0=msk, scalar=NEG, in1=I_,
                       op0=ALU.mult, op1=ALU.add)
                    OP(P, 'dve', 'tensor_tensor', ['bis'], ['bis'], out=bis[:, 1:2], in0=bis[:, 5:6], in1=bis[:, 0:1],
                       op=ALU.subtract)
                    OP(P, 'dve', 'tensor_scalar', ['bis'], ['bis'], out=bis[:, 1:2], in0=bis[:, 1:2], scalar1=1.0001,
                       scalar2=1e-6, op0=ALU.mult, op1=ALU.add)
                    for it in range(N_BISECT):
                        OP(P, 'dve', 'tensor_scalar', ['bis'], ['bis'], out=bis[:, 1:2], in0=bis[:, 1:2], scalar1=0.5,
                           scalar2=None, op0=ALU.mult)
                        OP(P, 'dve', 'tensor_tensor', ['bis'], ['bis'], out=bis[:, 2:3], in0=bis[:, 0:1], in1=bis[:, 1:2],
                           op=ALU.add)
                        OP(P, 'dve', 'tensor_scalar', ['I', 'bis', 'M'], ['M', 'bis'], out=M_, in0=I_, scalar1=bis[:, 2:3],
                           scalar2=0.0, op0=ALU.is_ge, op1=ALU.add, accum_out=bis[:, 3:4])
                        OP(P, 'dve', 'tensor_tensor', ['bis', 'tabs'], ['bis'], out=bis[:, 4:5], in0=bis[:, 3:4],
                           in1=kq_[:, jb:jb + 1], op=ALU.is_ge)
                        OP(P, 'dve', 'scalar_tensor_tensor', ['bis'], ['bis'], out=bis[:, 0:1], in0=bis[:, 1:2],
                           scalar=bis[:, 4:5], in1=bis[:, 0:1], op0=ALU.mult, op1=ALU.add)
                    OP(P, 'dve', 'tensor_scalar', ['I', 'bis', 'M'], ['M'], out=M_, in0=I_, scalar1=bis[:, 0:1], scalar2=None,
                       op0=ALU.is_ge)
                    for i in range(NKB):
                        transpose_out(M_[:, i * 128:(i + 1) * 128], 'M', 128, MT[:, i * 128:(i + 1) * 128], 'MT', eng='dve')
                    attend_ab(b, kaT, 'kaT', va, 'va', 0, None)
                gate_z('za', 0, 12)
            if 'B' in phases:
                q_fm('qb', 12)
                for b in range(2):
                    jb = qt * 2 + b
                    P.dma('sp', selb[:, :], selbc_d[:, jb * NKB * 12:(jb + 1) * NKB * 12], writes=['selb'])
                    OP(P, 'dve', 'tensor_tensor', ['negc', 'selb'], ['tmpc'], out=tmpc[:, :, :],
                       in0=negc[:, :, :], in1=selb[:, :].rearrange("p (i h) -> p i h", h=12), op=ALU.mult)
                    pb2 = nxt('pa', 2)
                    for i in range(NKB):
                        OP(P, 'pe', 'matmul', ['tmpc', 'cs'], ['pa%d' % pb2], pa[pb2][:, 0:12], lhsT=E64_, rhs=tmpc[:, i, :],
                           start=(i == 0), stop=(i == NKB - 1))
                    OP(P, 'act', 'copy', ['pa%d' % pb2], ['cbn'], out=cbn[:, :], in_=pa[pb2][:, 0:12])
                    for i in range(NKB):
                        OP(P, 'dve', 'tensor_tensor', ['negc', 'cbn'], ['fbias'], out=fbias[:, i, :], in0=negc[:, i, :],
                           in1=cbn[:, :], op=ALU.subtract)
                    OP(P, 'dve', 'tensor_scalar', ['fbias'], ['fbias'], out=fbias[:, :, :], in0=fbias[:, :, :], scalar1=45.0,
                       scalar2=None, op0=ALU.min)
                    for i in range(NKB):
                        OP(P, 'dve', 'tensor_scalar', ['qrow', 'tabs'], ['MT'], out=MT[:, i * 128:(i + 1) * 128],
                           in0=qrow[:, b * 128:(b + 1) * 128], scalar1=kpos_[:, i:i + 1], scalar2=None, op0=ALU.is_ge)
                    attend_ab(b, kbT, 'kbT', vb, 'vb', 12, fbias)
                gate_z('zb', 12, 12)
            if 'C' in phases:
                q_fm('qc', 8)
                for b in range(2):
                    for h in range(4):
                        pb = nxt('pm', 4)
                        for mb in range(2):
                            for dc in range(2):
                                OP(P, 'pe', 'matmul', ['mkT', 'qT'], ['pm%d' % pb], pm[pb][:, mb * 128:(mb + 1) * 128],
                                   lhsT=mkT[:, 2 * h + dc, mb * 128:(mb + 1) * 128], rhs=qT[:, 2 * h + dc, b * 128:(b + 1) * 128],
                                   start=(dc == 0), stop=(dc == 1))
                        y = nxt('PT', 2)
                        OP(P, 'act', 'activation', ['pm%d' % pb], ['PT%d' % y], out=PT[y][:, 0:256], in_=pm[pb][:, 0:256],
                           func=AF.Exp, scale=SM_SCALE_C)
                        pab = nxt('pa', 2)
                        for mb in range(2):
                            OP(P, 'pe', 'matmul', ['PT%d' % y, 'mv'], ['pa%d' % pab], pa[pab][:, 0:257],
                               lhsT=PT[y][:, mb * 128:(mb + 1) * 128], rhs=mv[:, mb, h, 0:257], start=(mb == 0), stop=(mb == 1))
                        softmax_out([(pa[pab], 'pa%d' % pab)], 1, 256, 24 + 2 * h, b)
                gate_z('zc', 24, 8)
            if DBG.get('dump') == 1 and qt == 0:
                for c in range(32):
                    o = nxt('ost', 3)
                    OP(P, 'act', 'copy', ['ozT'], ['ost%d' % o], out=ost[o][:, 0:T], in_=ozT[:, c, :])
                    P.dma('act', odbg[c * 128:(c + 1) * 128, :], ost[o][:, 0:T], reads=['ost%d' % o], is_output=True)
            if 'D' in phases:
                phase_D()
            if 'E' in phases:
                for b in range(2):
                    jb = qt * 2 + b
                    phase_E(b, xq[jb * 128:(jb + 1) * 128, :], oy[jb * 128:(jb + 1) * 128, :])
        if DBG.get('sample', True):
            P.barrier()
            U32 = mybir.dt.uint32
            I32 = mybir.dt.int32
            SA = kaT[:, :, :].rearrange("p a b -> p (a b)").bitcast(F32)
            SB_ = kbT[:, :, :].rearrange("p a b -> p (a b)").bitcast(F32)
            SV = va[:, :, :, :].rearrange("p a b c -> p (a b c)").bitcast(F32)
            SW = vb[:, :, :, :].rearrange("p a b c -> p (a b c)").bitcast(F32)
            Kpg = [SA[:, 0:512], SA[:, 512:1024]]
            Vpg = [SA[:, 1024:1536], SA[:, 1536:2048]]
            Lg = SA[:, 2048:2816]
            Cc = SA[:, 2816:3584]
            Ipg = [SA[:, 3584:3648], SA[:, 3648:3712]]
            fbp = SB_[:, 0:780]
            pref = SB_[:, 780:1560]
            Tt = SB_[:, 1560:2328]
            I_s = SB_[:, 2328:2393]
            sel = SB_[:, 2400:2465]
            rowf = SB_[:, 2472:2536]
            rowi = SB_[:, 2536:2600].bitcast(U32)
            ptf = SB_[:, 2600:2664]
            ptbi = SB_[:, 2664:2920].bitcast(I32)
            kbfp = SB_[:, 2944:3200].bitcast(BF16)
            kTp = [SB_[:, 3200:3456].bitcast(BF16), SB_[:, 3456:3712].bitcast(BF16)]
            vbfp = [SV[:, 0:260].bitcast(BF16), SV[:, 260:520].bitcast(BF16)]
            ikd = SV[:, 520:584].bitcast(BF16)
            ikTp = SV[:, 584:648].bitcast(BF16)
            wbs = SV[:, 648:664]
            Rs = SV[:, 664:680]
            lfb = SV[:, 680:692]
            cT = SV[:, 692:704]
            bsm = SV[:, 704:712]
            Sx = SV[:, 712:724]
            PTs = SV[:, 724:730].bitcast(BF16)
            osm = SV[:, 736:1264]
            onbs = SV[:, 1264:1776].bitcast(BF16)
            tot12 = SV[:, 1776:1788]
            kaTs = SW[:, 0:256].bitcast(BF16)
            kbTs = SW[:, 256:512].bitcast(BF16)
            vas = SW[:, 512:772].bitcast(BF16)
            vbs = SW[:, 772:1032].bitcast(BF16)
            ik2Ts = SW[:, 1032:1096].bitcast(BF16)
            logfs = SW[:, 1096:1108]
            smc = SW[:, 1200:1848]
            ER = lambda r: smc[:, r * 128:(r + 1) * 128]
            ONES = smc[:, 512:640]
            OH = lambda r: smc[:, 640 + r:641 + r]
            OHN = lambda r: smc[:, 644 + r:645 + r]

            P.dma('sp', smc, smc_d, writes=['smc'])
            P.dma('sp', ptbi, ptb_d, writes=['ptb'])
            for v_ in vbfp + [vas, vbs]:
                OP(P, 'dve', 'memset', [], ['vones'], v_.rearrange("p (n c) -> p n c", c=130)[:, :, 128:129], 1.0)
            norm_block(xsm, 0, 0)
            kv_tile(None, 1, 0, osamp, rk, 'rk', NKB, False,
                    smp=dict(ka=kaTs, kb=kbTs, va=vas, vb=vbs, ik=ik2Ts, logf=logfs))
            jbs = NQB
            rqs = lambda a, b_: rk[:, NKB * 48 + a:NKB * 48 + b_]

            def proj_q_tm(seg, nch, dstT, dkey, rots):
                for g in range(nch):
                    wi = wload([(w_in, SEGS[seg][0] + g * 128, 128, 0)], KC)
                    pb = mm_tm(wi, 0, 128)
                    o = evac_tm(pb, 128)
                    for (off, half, ca, sa_) in rots:
                        rot(o, off, half, rqs(*ca), rqs(*sa_), 'rk')
                    OP(P, 'dve', 'tensor_copy', ['ost%d' % o], ['kbf'], out=kbf[:, :], in_=ost[o][:, 0:128])
                    transpose_out(kbf[:, :], 'kbf', 128, dstT[:, g, 0:128], dkey)

            def rowidx(r):
                OP(P, 'dve', 'tensor_copy', ['ptb'], ['ptf'], out=ptf, in_=ptbi[:, r * 64:(r + 1) * 64])
                OP(P, 'dve', 'tensor_scalar', ['ptf', 'tabs'], ['rowf'], out=rowf, in0=ptf, scalar1=128.0, scalar2=kpos_[:, 0:1],
                   op0=ALU.mult, op1=ALU.add)
                OP(P, 'dve', 'tensor_copy', ['rowf'], ['rowi'], out=rowi, in_=rowf)

            def bcast_rows(r):
                pb2 = nxt('pa', 2)
                OP(P, 'pe', 'matmul', ['smc', 'iw'], ['pa%d' % pb2], pa[pb2][:, 0:16], lhsT=ER(r), rhs=iw[:, 0, :], start=True, stop=True)
                OP(P, 'pe', 'matmul', ['smc', 'slogf'], ['pa%d' % pb2], pa[pb2][:, 16:28], lhsT=ER(r), rhs=logfs, start=True, stop=True)
                OP(P, 'act', 'copy', ['pa%d' % pb2], ['wbs'], out=wbs, in_=pa[pb2][:, 0:16])
                OP(P, 'act', 'copy', ['pa%d' % pb2], ['lfb'], out=lfb, in_=pa[pb2][:, 16:28])

            def index_scores(r):
                for pg in range(65):
                    if pg < 64:
                        k = pg % 2
                        P.idma(Ipg[k], cai, rowi[:, pg:pg + 1], reads=['rowi'], writes=['Ipg%d' % k])
                        OP(P, 'dve', 'tensor_copy', ['Ipg%d' % k], ['ikd'], out=ikd[:, 0:64], in_=Ipg[k])
                        OP(P, 'dve', 'tensor_copy', ['Ipg%d' % k], ['ikd'], out=ikd[:, 64:128], in_=Ipg[k])
                        transpose_out(ikd, 'ikd', 128, ikTp, 'ikTp')
                        kt_, kk_ = ikTp, 'ikTp'
                    else:
                        kt_, kk_ = ik2Ts, 'sik'
                    pb = nxt('pmS', 2)
                    for h in range(16):
                        g, hh = h // 2, h % 2
                        OP(P, 'pe', 'matmul', [kk_, 'iqT'], ['pm%d' % pb], pm[pb][:, h:h + 1], lhsT=kt_[hh * 64:(hh + 1) * 64, :],
                           rhs=iqT[hh * 64:(hh + 1) * 64, g, r:r + 1], start=True, stop=True)
                    OP(P, 'act', 'activation', ['pm%d' % pb], ['Rs'], out=Rs, in_=pm[pb][:, 0:16], func=AF.Relu)
                    OP(P, 'dve', 'tensor_tensor', ['Rs', 'wbs'], ['Rs'], out=Rs, in0=Rs, in1=wbs, op=ALU.mult)
                    OP(P, 'dve', 'tensor_reduce', ['Rs'], ['I_s'], out=I_s[:, pg:pg + 1], in_=Rs, axis=mybir.AxisListType.X, op=ALU.add)
                OP(P, 'dve', 'tensor_tensor', ['I_s', 'smc'], ['I_s'], out=I_s[:, 64:65], in0=I_s[:, 64:65], in1=OHN(r), op=ALU.add)
                OP(P, 'dve', 'memset', [], ['bsm'], bsm[:, 0:1], -64.0)
                OP(P, 'dve', 'memset', [], ['bsm'], bsm[:, 1:2], 128.0)
                for it in range(N_BISECT_S):
                    OP(P, 'dve', 'tensor_scalar', ['bsm'], ['bsm'], out=bsm[:, 1:2], in0=bsm[:, 1:2], scalar1=0.5, scalar2=None, op0=ALU.mult)
                    OP(P, 'dve', 'tensor_tensor', ['bsm'], ['bsm'], out=bsm[:, 2:3], in0=bsm[:, 0:1], in1=bsm[:, 1:2], op=ALU.add)
                    OP(P, 'dve', 'tensor_scalar', ['I_s', 'bsm'], ['sel', 'bsm'], out=sel, in0=I_s, scalar1=bsm[:, 2:3], scalar2=0.0,
                       op0=ALU.is_ge, op1=ALU.add, accum_out=bsm[:, 3:4])
                    pb2 = nxt('pa', 2)
                    OP(P, 'pe', 'matmul', ['smc', 'bsm'], ['pa%d' % pb2], pa[pb2][:, 0:1], lhsT=ONES, rhs=bsm[:, 3:4], start=True, stop=True)
                    OP(P, 'act', 'copy', ['pa%d' % pb2], ['bsm2'], out=bsm[:, 5:6], in_=pa[pb2][:, 0:1])
                    OP(P, 'dve', 'tensor_scalar', ['bsm2'], ['bsm'], out=bsm[:, 4:5], in0=bsm[:, 5:6], scalar1=float(TOPK), scalar2=None, op0=ALU.is_ge)
                    OP(P, 'dve', 'scalar_tensor_tensor', ['bsm'], ['bsm'], out=bsm[:, 0:1], in0=bsm[:, 1:2], scalar=bsm[:, 4:5], in1=bsm[:, 0:1],
                       op0=ALU.mult, op1=ALU.add)
                OP(P, 'dve', 'tensor_scalar', ['I_s', 'bsm'], ['sel'], out=sel, in0=I_s, scalar1=bsm[:, 0:1], scalar2=None, op0=ALU.is_ge)

            def fox_bias(r):
                for pg in range(64):
                    P.idma(Lg[:, pg * 12:(pg + 1) * 12], cbl, rowi[:, pg:pg + 1], reads=['rowi'], writes=['Lg'])
                for hf in range(2):
                    pb = nxt('pmS', 2)
                    OP(P, 'pe', 'matmul', ['Lg', 'cs'], ['pm%d' % pb], pm[pb][:, 0:384], lhsT=U_, rhs=Lg[:, hf * 384:(hf + 1) * 384], start=True, stop=True)
                    OP(P, 'act', 'copy', ['pm%d' % pb], ['Cc'], out=Cc[:, hf * 384:(hf + 1) * 384], in_=pm[pb][:, 0:384])
                    pb = nxt('pmS', 2)
                    OP(P, 'pe', 'matmul', ['Lg', 'smc'], ['pm%d' % pb], pm[pb][:, 0:384], lhsT=ONES, rhs=Lg[:, hf * 384:(hf + 1) * 384], start=True, stop=True)
                    OP(P, 'act', 'copy', ['pm%d' % pb], ['Tt'], out=Tt[:, hf * 384:(hf + 1) * 384], in_=pm[pb][:, 0:384])
                OP(P, 'dve', 'tensor_reduce', ['Tt'], ['tot12'], out=tot12, in_=Tt.rearrange("p (j h) -> p h j", h=12),
                   axis=mybir.AxisListType.X, op=ALU.add)
                OP(P, 'dve', 'tensor_tensor', ['tot12', 'lfb'], ['cT'], out=cT, in0=tot12, in1=lfb, op=ALU.add)
                OP(P, 'dve', 'tensor_scalar', ['cT'], ['pref'], out=pref[:, 0:12], in0=cT, scalar1=-1.0, scalar2=None, op0=ALU.mult)
                for j in range(1, 64):
                    OP(P, 'dve', 'tensor_tensor', ['pref', 'Tt'], ['pref'], out=pref[:, j * 12:(j + 1) * 12], in0=pref[:, (j - 1) * 12:j * 12],
                       in1=Tt[:, (j - 1) * 12:j * 12], op=ALU.add)
                OP(P, 'dve', 'tensor_tensor', ['Cc', 'pref'], ['Cc'], out=Cc, in0=Cc, in1=pref[:, 0:768], op=ALU.add)
                OP(P, 'dve', 'tensor_scalar', ['Cc'], ['fbp'], out=fbp[:, 0:768], in0=Cc, scalar1=-1.0, scalar2=None, op0=ALU.mult)
                OP(P, 'dve', 'memset', [], ['fbp'], fbp[:, 768:780], 0.0)

            def attend_sample(r, branch):
                poolk, poolv = (cak, cav) if branch == 'A' else (cbk, cbv)
                kTs_, vs_ = (kaTs, vas) if branch == 'A' else (kbTs, vbs)
                kself, vself = ('ska', 'sva') if branch == 'A' else ('skb', 'svb')
                ch0 = 0 if branch == 'A' else 12
                accs = [(pa[0], 'pa0'), (pa[1], 'pa1'), (pm[2], 'pm2'), (pm[3], 'pm3')]
                for pg in range(65):
                    if pg < 64:
                        k = pg % 2
                        P.idma(Kpg[k], poolk, rowi[:, pg:pg + 1], reads=['rowi'], writes=['Kpg%d' % k])
                        P.idma(Vpg[k], poolv, rowi[:, pg:pg + 1], reads=['rowi'], writes=['Vpg%d' % k])
                        OP(P, 'dve', 'tensor_copy', ['Kpg%d' % k], ['kbfp'], out=kbfp, in_=Kpg[k])
                        c = cnt['pt']
                        cnt['pt'] += 8 - (c % 8) if (c % 8) > 4 else 0
                        bank = (cnt['pt'] // 8) % 2
                        s0 = cnt['pt'] % 8
                        cnt['pt'] += 4
                        for n in range(4):
                            OP(P, 'pe', 'transpose', ['kbfp', 'ident_b'], ['pt%d' % bank], out=pt[bank][:, (s0 + n) * 128:(s0 + n + 1) * 128],
                               in_=kbfp[:, n * 128:(n + 1) * 128], identity=ident_b[:, :])
                        OP(P, 'act', 'copy', ['pt%d' % bank], ['kTp%d' % k], out=kTp[k], in_=pt[bank][:, s0 * 128:(s0 + 4) * 128])
                        OP(P, 'pool', 'tensor_copy', ['Vpg%d' % k, 'vones'], ['vbfp%d' % k], out=vbfp[k].rearrange("p (n c) -> p n c", c=130)[:, :, 0:128],
                           in_=Vpg[k].rearrange("p (n c) -> p n c", c=128))
                        kt_, kk_, vt_, vk_ = kTp[k], 'kTp%d' % k, vbfp[k], 'vbfp%d' % k
                    else:
                        kt_, kk_, vt_, vk_ = kTs_, kself, vs_, vself
                    pb = nxt('pmS', 2)
                    for n in range(4):
                        OP(P, 'pe', 'matmul', [kk_, 'qT'], ['pm%d' % pb], pm[pb][:, 3 * n:3 * n + 3], lhsT=kt_[:, n * 128:(n + 1) * 128],
                           rhs=qT[:, 3 * n:3 * n + 3, r], start=True, stop=True)
                    if branch == 'A':
                        OP(P, 'act', 'activation', ['pm%d' % pb], ['PTs'], out=PTs, in_=pm[pb][:, 0:12], func=AF.Exp, scale=SM_SCALE)
                        OP(P, 'dve', 'tensor_scalar', ['PTs', 'sel'], ['PTs'], out=PTs, in0=PTs, scalar1=sel[:, pg:pg + 1], scalar2=None, op0=ALU.mult)
                    else:
                        OP(P, 'act', 'activation', ['pm%d' % pb], ['Sx'], out=Sx, in_=pm[pb][:, 0:12], func=AF.Copy, scale=SM_SCALE)
                        OP(P, 'dve', 'tensor_tensor', ['Sx', 'fbp'], ['Sx'], out=Sx, in0=Sx, in1=fbp[:, pg * 12:(pg + 1) * 12], op=ALU.add)
                        OP(P, 'act', 'activation', ['Sx'], ['PTs'], out=PTs, in_=Sx, func=AF.Exp)
                        if pg == 64:
                            OP(P, 'dve', 'tensor_scalar', ['PTs', 'smc'], ['PTs'], out=PTs, in0=PTs, scalar1=OH(r), scalar2=None, op0=ALU.mult)
                    for n in range(4):
                        at, ak = accs[n]
                        OP(P, 'pe', 'matmul', ['PTs', vk_, 'vones'], [ak], at[0:3, 0:129], lhsT=PTs[:, 3 * n:3 * n + 3],
                           rhs=vt_[:, n * 130:n * 130 + 129], start=(pg == 0), stop=(pg == 64))
                for n in range(4):
                    at, ak = accs[n]
                    OP(P, 'act', 'copy', [ak], ['osm'], out=osm[0:3, n * 130:n * 130 + 129], in_=at[0:3, 0:129])
                for n in range(4):
                    OP(P, 'dve', 'reciprocal', ['osm'], ['rcp'], out=rcp[0:3, 0:1], in_=osm[0:3, n * 130 + 128:n * 130 + 129])
                    OP(P, 'dve', 'tensor_scalar', ['osm', 'rcp'], ['onbs'], out=onbs[0:3, n * 128:(n + 1) * 128], in0=osm[0:3, n * 130:n * 130 + 128],
                       scalar1=rcp[0:3, 0:1], scalar2=None, op0=ALU.mult)
                    c = cnt['pt']
                    cnt['pt'] += 1
                    bank, slot = (c // 8) % 2, c % 8
                    OP(P, 'pe', 'transpose', ['onbs', 'ident_b'], ['pt%d' % bank], out=pt[bank][:, slot * 128:slot * 128 + 3],
                       in_=onbs[0:3, n * 128:(n + 1) * 128], identity=ident_b[0:3, 0:3])
                    OP(P, 'act', 'copy', ['pt%d' % bank], ['ozT'], out=ozT[:, ch0 + 3 * n:ch0 + 3 * n + 3, r], in_=pt[bank][:, slot * 128:slot * 128 + 3])

            def attend_sample_c(r):
                for mb in range(2):
                    for (src, isk) in ((cmk, True), (cmv, False)):
                        for hf in range(2):
                            k = hf
                            P.dma('sp', Kpg[k], src[r * N_MEM + mb * 128:r * N_MEM + (mb + 1) * 128, hf * 512:(hf + 1) * 512], writes=['Kpg%d' % k])
                            if isk:
                                OP(P, 'dve', 'tensor_copy', ['Kpg%d' % k], ['kbfp'], out=kbfp, in_=Kpg[k])
                                for n in range(4):
                                    transpose_out(kbfp[:, n * 128:(n + 1) * 128], 'kbfp', 128, mkT[:, hf * 4 + n, mb * 128:(mb + 1) * 128], 'mkT')
                            else:
                                for hh in range(2):
                                    OP(P, 'dve', 'tensor_copy', ['Kpg%d' % k], ['mv'], out=mv[:, mb, hf * 2 + hh, 0:256], in_=Kpg[k][:, hh * 256:(hh + 1) * 256])
                accs = [(pa[0], 'pa0'), (pa[1], 'pa1'), (pm[2], 'pm2'), (pm[3], 'pm3')]
                for h in range(4):
                    pb = nxt('pmS', 2)
                    for mb in range(2):
                        for dc in range(2):
                            OP(P, 'pe', 'matmul', ['mkT', 'qT'], ['pm%d' % pb], pm[pb][:, mb:mb + 1], lhsT=mkT[:, 2 * h + dc, mb * 128:(mb + 1) * 128],
                               rhs=qT[:, 2 * h + dc, r:r + 1], start=(dc == 0), stop=(dc == 1))
                    OP(P, 'act', 'activation', ['pm%d' % pb], ['PTs'], out=PTs[:, 0:2], in_=pm[pb][:, 0:2], func=AF.Exp, scale=SM_SCALE_C)
                    at, ak = accs[h]
                    for mb in range(2):
                        OP(P, 'pe', 'matmul', ['PTs', 'mv'], [ak], at[0:1, 0:257], lhsT=PTs[:, mb:mb + 1], rhs=mv[:, mb, h, 0:257],
                           start=(mb == 0), stop=(mb == 1))
                    OP(P, 'act', 'copy', [ak], ['osm'], out=osm[0:1, 0:257], in_=at[0:1, 0:257])
                    OP(P, 'dve', 'reciprocal', ['osm'], ['rcp'], out=rcp[0:1, 0:1], in_=osm[0:1, 256:257])
                    OP(P, 'dve', 'tensor_scalar', ['osm', 'rcp'], ['onbs'], out=onbs[0:1, 0:256], in0=osm[0:1, 0:256], scalar1=rcp[0:1, 0:1],
                       scalar2=None, op0=ALU.mult)
                    for dc in range(2):
                        c = cnt['pt']
                        cnt['pt'] += 1
                        bank, slot = (c // 8) % 2, c % 8
                        OP(P, 'pe', 'transpose', ['onbs', 'ident_b'], ['pt%d' % bank], out=pt[bank][:, slot * 128:slot * 128 + 1],
                           in_=onbs[0:1, dc * 128:(dc + 1) * 128], identity=ident_b[0:1, 0:1])
                        OP(P, 'act', 'copy', ['pt%d' % bank], ['ozT'], out=ozT[:, 24 + 2 * h + dc, r:r + 1], in_=pt[bank][:, slot * 128:slot * 128 + 1])

            nseq = DBG.get('nseq', SAMP_PER_CORE)
            sphases = DBG.get('sphases', 'ABCDE')
            if 'A' in sphases:
                proj_q_tm('qa', 12, qT, 'qT', [(0, 16, (0, 16), (16, 32))])
                proj_q_tm('iq', 8, iqT, 'iqT', [(0, 8, (32, 40), (40, 48)), (64, 8, (32, 40), (40, 48))])
                wi = wload([(w_in, SEGS['iw'][0], 16, 0)], KC)
                pb = mm_tm(wi, 0, 16)
                OP(P, 'act', 'activation', ['pm%d' % pb], ['iw'], out=iw[:, 0, :], in_=pm[pb][:, 0:16], func=AF.Copy,
                   scale=(64 ** -0.5) * (16 ** -0.5))
                for r in range(nseq):
                    rowidx(r)
                    bcast_rows(r)
                    index_scores(r)
                    attend_sample(r, 'A')
                gate_z('za', 0, 12)
            if 'B' in sphases:
                q_fm('qb', 12)
                for r in range(nseq):
                    rowidx(r)
                    bcast_rows(r)
                    fox_bias(r)
                    attend_sample(r, 'B')
                gate_z('zb', 12, 12)
            if 'C' in sphases:
                q_fm('qc', 8)
                for r in range(nseq):
                    attend_sample_c(r)
                gate_z('zc', 24, 8)
            if DBG.get('dump') == 's':
                for c in range(32):
                    o = nxt('ost', 3)
                    OP(P, 'act', 'copy', ['ozT'], ['ost%d' % o], out=ost[o][:, 0:T], in_=ozT[:, c, :])
                    P.dma('act', odbg[c * 128:(c + 1) * 128, :], ost[o][:, 0:T], reads=['ost%d' % o], is_output=True)
            if 'D' in sphases:
                phase_D()
            if 'E' in sphases:
                phase_E(0, xsm, oys)

        P.finish()
        P.emit()
    return nc


def _rot_tables(pos, rot):
    half = rot // 2
    inv = np.power(np.float32(ROPE_THETA), -np.arange(half, dtype=np.float32) * np.float32(2.0 / rot)).astype(np.float32)
    ang = pos.astype(np.float32)[:, None] * inv[None, :]
    return np.cos(ang).astype(np.float32), np.sin(ang).astype(np.float32)


def _rot_pack(pos):
    nb = pos.shape[0] // 128
    c16, s16 = _rot_tables(pos, 32)
    c8, s8 = _rot_tables(pos, 16)
    tab = np.concatenate([c16, s16, c8, s8], axis=1).astype(np.float32)
    return np.ascontiguousarray(tab.reshape(nb, 128, 48).transpose(1, 0, 2).reshape(128, nb * 48))


def make_shared(inp):
    f32 = np.float32
    sh = {}
    sh['w_in'] = np.ascontiguousarray(np.asarray(inp['w_in'], f32)[0])
    sh['w_bra'] = np.ascontiguousarray(np.asarray(inp['w_br_a'], f32)[0])
    sh['w_brb'] = np.ascontiguousarray(np.asarray(inp['w_br_b'], f32)[0])
    sh['w_brc'] = np.ascontiguousarray(np.asarray(inp['w_br_c'], f32)[0])
    sh['w_o'] = np.ascontiguousarray(np.asarray(inp['w_out'], f32)[0])
    sh['w_mem'] = np.ascontiguousarray(np.asarray(inp['w_mem_kv'], f32)[0])
    sh['gcols'] = np.ascontiguousarray(np.concatenate(
        [np.asarray(inp['g_norm'], f32)[0].reshape(KC, 128).T, np.asarray(inp['g_mem'], f32)[0].reshape(KC, 128).T], axis=1))
    sh['gfin'] = np.ascontiguousarray(np.broadcast_to(np.asarray(inp['g_final'], f32)[None, :], (128, D_MODEL)))
    sh['bfrow'] = np.ascontiguousarray(np.broadcast_to(np.asarray(inp['b_forget'], f32)[0][None, :], (128, 12)))
    p = np.arange(128)
    ident = np.eye(128, dtype=f32)
    U = (p[:, None] <= p[None, :]).astype(f32)
    E127 = np.zeros((128, 128), f32); E127[127, :] = 1
    E64 = np.zeros((128, 128), f32); E64[64, :] = 1
    sh['cst'] = np.ascontiguousarray(np.concatenate([ident, U, E127, E64], axis=1))
    sh['kposrow'] = np.ascontiguousarray(np.broadcast_to(np.arange(SEQ, dtype=f32)[None, :], (128, SEQ)))
    return sh


def make_core_inputs(c, xp_b, xsamp, mem_b, sh, pools=None):
    f32 = np.float32
    half = c % 2
    m = dict(sh)
    m['xk'] = np.ascontiguousarray(xp_b)
    m['xq'] = np.ascontiguousarray(xp_b[half * 1024:(half + 1) * 1024])
    xsm = np.zeros((128, D_MODEL), f32)
    xsm[:SAMP_PER_CORE] = xsamp[c * SAMP_PER_CORE:(c + 1) * SAMP_PER_CORE]
    m['xsm'] = xsm
    m['xmem'] = np.ascontiguousarray(mem_b)
    m['rotk'] = _rot_pack(np.concatenate([np.arange(SEQ), np.full(128, PAST_LEN)]))
    qp = half * 1024 + np.arange(1024)
    m['rotq'] = _rot_pack(qp)
    p = np.arange(128)
    qpos = (half * 1024 + np.arange(NQB)[None, :] * 128 + p[:, None]).astype(f32)
    kq = np.minimum(TOPK, qpos + 1).astype(f32)
    kpos = (np.arange(NKB)[None, :] * 128 + p[:, None]).astype(f32)
    m['tabs'] = np.ascontiguousarray(np.concatenate([kq, kpos, qpos], axis=1))
    m['qrow'] = np.ascontiguousarray(np.broadcast_to(qp.astype(f32)[None, :], (128, 1024)))
    sel = np.zeros((128, NQB, NKB, 12), f32)
    for j in range(NQB):
        sel[:, j, half * 8 + j, :] = 1
    m['selbc'] = np.ascontiguousarray(sel.reshape(128, -1))
    if pools is not None:
        for k in ('cak', 'cav', 'cbk', 'cbv', 'cai', 'cbl'):
            m[k] = pools[k]
        s0 = c * SAMP_PER_CORE
        m['cmk'] = np.ascontiguousarray(pools['cmk'][s0:s0 + SAMP_PER_CORE].reshape(SAMP_PER_CORE * N_MEM, 1024))
        m['cmv'] = np.ascontiguousarray(pools['cmv'][s0:s0 + SAMP_PER_CORE].reshape(SAMP_PER_CORE * N_MEM, 1024))
        pt = np.asarray(pools['pt'])[s0:s0 + SAMP_PER_CORE].astype(np.int32).reshape(1, -1)
        m['ptb'] = np.ascontiguousarray(np.broadcast_to(pt, (128, SAMP_PER_CORE * 64)))
        smc = np.zeros((128, 4 * 128 + 128 + 8), f32)
        for r in range(4):
            smc[r, r * 128:(r + 1) * 128] = 1.0
            smc[r, 640 + r] = 1.0
            smc[:, 644 + r] = NEG
            smc[r, 644 + r] = 0.0
        smc[:, 512:640] = 1.0
        m['smc'] = smc
    return m


def make_pools(cache_a_k, cache_a_v, cache_a_idx, cache_b_k, cache_b_v, cache_b_logf, cache_mem_k, cache_mem_v, page_table):
    f32 = np.float32
    rows = N_POOL * 128
    return dict(cak=np.asarray(cache_a_k, f32).reshape(rows, 512), cav=np.asarray(cache_a_v, f32).reshape(rows, 512),
                cbk=np.asarray(cache_b_k, f32).reshape(rows, 512), cbv=np.asarray(cache_b_v, f32).reshape(rows, 512),
                cai=np.asarray(cache_a_idx, f32).reshape(rows, 64), cbl=np.asarray(cache_b_logf, f32).reshape(rows, 12),
                cmk=np.asarray(cache_mem_k, f32)[0].reshape(DEC_BATCH, N_MEM, 1024),
                cmv=np.asarray(cache_mem_v, f32)[0].reshape(DEC_BATCH, N_MEM, 1024), pt=np.asarray(page_table))


_NC_CACHE = {}


def kernel(x_prompt, x_sample, cache_a_k, cache_a_v, cache_a_idx, cache_b_k, cache_b_v, cache_b_logf,
           cache_mem_k, cache_mem_v, page_table, mem_prompt, g_norm, w_in, b_forget, w_br_a, w_br_b,
           w_br_c, w_out, g_mem, w_mem_kv, g_final):
    f32 = np.float32
    inp = dict(w_in=w_in, w_br_a=w_br_a, w_br_b=w_br_b, w_br_c=w_br_c, w_out=w_out, w_mem_kv=w_mem_kv, g_norm=g_norm,
               g_mem=g_mem, g_final=g_final, b_forget=b_forget)
    sh = make_shared(inp)
    xp = np.asarray(x_prompt, f32)
    xsamp = np.asarray(x_sample, f32).reshape(DEC_BATCH, D_MODEL)
    memp = np.asarray(mem_prompt, f32)
    if 'nc' not in _NC_CACHE:
        _NC_CACHE['nc'] = build_nc()
    nc = _NC_CACHE['nc']
    pools = make_pools(cache_a_k, cache_a_v, cache_a_idx, cache_b_k, cache_b_v, cache_b_logf, cache_mem_k, cache_mem_v, page_table)
    in_maps = [make_core_inputs(c, xp[c // 2], xsamp, memp[c // 2], sh, pools) for c in range(NCORES)]
    res = run_bass_kernel_spmd(nc, in_maps, core_ids=list(range(NCORES)))
    R = [{k: np.asarray(v) for k, v in r.items()} for r in res.results]
    y_prompt = np.stack([np.concatenate([R[2 * b]['oy'], R[2 * b + 1]['oy']], axis=0) for b in range(BATCH)]).astype(f32)
    pk = np.stack([R[2 * b]['okv'] for b in range(BATCH)])
    sk = np.concatenate([R[c]['osamp'][:SAMP_PER_CORE] for c in range(NCORES)], axis=0)
    om = np.stack([R[2 * b]['omem'] for b in range(BATCH)])

    def pr(c0, n, shape):
        return np.ascontiguousarray(pk[:, :, c0:c0 + n]).reshape(shape).astype(f32)

    def sr(c0, n, shape):
        return np.ascontiguousarray(sk[:, c0:c0 + n]).reshape(shape).astype(f32)

    p_ak = pr(C_KA, 512, (1, BATCH, SEQ, 4, 128))
    p_av = pr(C_VA, 512, (1, BATCH, SEQ, 4, 128))
    p_ai = pr(C_IK, 64, (1, BATCH, SEQ, 64))
    p_bk = pr(C_KB, 512, (1, BATCH, SEQ, 4, 128))
    p_bv = pr(C_VB, 512, (1, BATCH, SEQ, 4, 128))
    p_bf = pr(C_FB, 12, (1, BATCH, SEQ, 12))
    p_mk = np.ascontiguousarray(om[:, :, :1024]).reshape(1, BATCH, N_MEM, 4, 256).astype(f32)
    p_mv = np.ascontiguousarray(om[:, :, 1024:]).reshape(1, BATCH, N_MEM, 4, 256).astype(f32)
    s_ak = sr(C_KA, 512, (1, DEC_BATCH, 1, 4, 128))
    s_av = sr(C_VA, 512, (1, DEC_BATCH, 1, 4, 128))
    s_ai = sr(C_IK, 64, (1, DEC_BATCH, 1, 64))
    s_bk = sr(C_KB, 512, (1, DEC_BATCH, 1, 4, 128))
    s_bv = sr(C_VB, 512, (1, DEC_BATCH, 1, 4, 128))
    s_bf = sr(C_FB, 12, (1, DEC_BATCH, 1, 12))
    y_sample = np.concatenate([R[c]['oys'][:SAMP_PER_CORE] for c in range(NCORES)], axis=0).reshape(DEC_BATCH, 1, D_MODEL).astype(f32)
    return (y_prompt, y_sample, p_ak, p_av, p_ai, p_bk, p_bv, p_bf, p_mk, p_mv,
            s_ak, s_av, s_ai, s_bk, s_bv, s_bf)
```

```python
import numpy as np
import concourse.bass as bass
import concourse.mybir as mybir
from concourse.bass_utils import run_bass_kernel_spmd

F32 = mybir.dt.float32
BF16 = mybir.dt.bfloat16
AF = mybir.ActivationFunctionType
ALU = mybir.AluOpType

D_MODEL = 4096
KC = D_MODEL // 128
BATCH, SEQ = 4, 2048
DEC_BATCH = 32
PAST_LEN = 8192
N_MEM = 256
EPS = 1e-6
ROPE_THETA = 500000.0
NCORES = 8
TOK_PER_CORE = BATCH * SEQ // NCORES
NPB = TOK_PER_CORE // 128
NBLK = NPB + 1
SAMP_PER_CORE = DEC_BATCH // NCORES
MEM_PER_CORE = BATCH * N_MEM // NCORES

WG = 128
KV_COLS = 512 * 4 + 64 + 12
C_KA, C_VA, C_KB, C_VB, C_IK, C_FB = 0, 512, 1024, 1536, 2048, 2112
IN_KA, IN_VA, IN_IK, IN_KB, IN_VB, IN_FB = 1536, 2048, 3584, 6736, 7248, 7760


class Prog:
    CH = 12000
    NDMA = 24

    def __init__(self, nc, stack):
        self.nc = nc
        self.stack = stack
        self.names = ['pe', 'act', 'dve', 'pool', 'sp']
        self.lists = {e: [] for e in self.names}
        self.count = {e: 0 for e in self.names}
        self.sems = {e: [] for e in self.names}
        self.seen = {e: {f: 0 for f in self.names} for e in self.names}
        self.dma_sems = [stack.enter_context(nc.semaphore("dq%d" % i)) for i in range(self.NDMA)]
        self.dma_uses = [0] * self.NDMA
        self.dma_seen = {e: [0] * self.NDMA for e in self.names}
        self.dma_next = 0
        self.lastw = {}
        self.readers = {}
        self.out_tokens = []

    def _sem(self, e, n):
        k = (n - 1) // self.CH
        while len(self.sems[e]) <= k:
            self.sems[e].append(self.stack.enter_context(self.nc.semaphore("c_%s_%d" % (e, len(self.sems[e])))))
        return self.sems[e][k], (n - 1) % self.CH + 1

    def _wait(self, e, tok, waits):
        if tok[0] == 'c':
            _, f, n = tok
            if f == e and e == 'pe':
                return
            if self.seen[e][f] >= n:
                return
            self.seen[e][f] = n
            sem, v = self._sem(f, n)
            waits.append((sem, v))
        else:
            _, idx, v = tok
            if self.dma_seen[e][idx] >= v:
                return
            self.dma_seen[e][idx] = v
            waits.append((self.dma_sems[idx], v))

    def _deps(self, e, reads, writes):
        waits = []
        deps = []
        for k in reads:
            if k in self.lastw:
                deps.append(self.lastw[k])
        for k in writes:
            if k in self.lastw:
                deps.append(self.lastw[k])
            deps.extend(self.readers.get(k, {}).values())
        for d in deps:
            self._wait(e, d, waits)
        return waits

    def _record(self, tok, reads, writes, ekey):
        for k in reads:
            self.readers.setdefault(k, {})[ekey] = tok
        for k in writes:
            self.lastw[k] = tok
            self.readers[k] = {}

    def op(self, e, fn, reads=(), writes=()):
        waits = self._deps(e, reads, writes)
        self.count[e] += 1
        n = self.count[e]
        sem, v = self._sem(e, n)
        self.lists[e].append((waits, fn, sem, 1))
        tok = ('c', e, n)
        self.seen[e][e] = max(self.seen[e][e], 0)
        self._record(tok, reads, writes, e)
        return tok

    def dma(self, q, out_ap, in_ap, reads=(), writes=(), is_output=False):
        idx = self.dma_next
        self.dma_next = (idx + 1) % self.NDMA
        waits = []
        if self.dma_uses[idx] > 0:
            self._wait(q, ('d', idx, 16 * self.dma_uses[idx]), waits)
        waits += self._deps(q, reads, writes)
        self.dma_uses[idx] += 1
        tok = ('d', idx, 16 * self.dma_uses[idx])
        fn = lambda eng, o=out_ap, i=in_ap: eng.dma_start(out=o, in_=i)
        self.lists[q].append((waits, fn, self.dma_sems[idx], 16))
        self._record(tok, reads, writes, ('dma', idx, self.dma_uses[idx]))
        if is_output:
            self.out_tokens.append(tok)
        return tok

    def barrier(self):
        for e in self.names:
            waits = []
            for f in self.names:
                if f != e and self.count[f] > 0:
                    self._wait(e, ('c', f, self.count[f]), waits)
            for idx in range(self.NDMA):
                if self.dma_uses[idx] > 0:
                    self._wait(e, ('d', idx, 16 * self.dma_uses[idx]), waits)
            self.lists[e].append((waits, None, None, 0))
        self.lastw = {}
        self.readers = {}

    def idma(self, out_ap, in_ap, idx_ap, reads=(), writes=()):
        q = 'pool'
        idx = self.dma_next
        self.dma_next = (idx + 1) % self.NDMA
        waits = []
        if self.dma_uses[idx] > 0:
            self._wait(q, ('d', idx, 16 * self.dma_uses[idx]), waits)
        waits += self._deps(q, reads, writes)
        self.dma_uses[idx] += 1
        tok = ('d', idx, 16 * self.dma_uses[idx])
        fn = lambda eng, o=out_ap, i=in_ap, x=idx_ap: eng.indirect_dma_start(
            out=o, out_offset=None, in_=i, in_offset=bass.IndirectOffsetOnAxis(x, 0))
        self.lists[q].append((waits, fn, self.dma_sems[idx], 16))
        self._record(tok, reads, writes, ('dma', idx, self.dma_uses[idx]))
        return tok

    def finish(self):
        waits = []
        for tok in self.out_tokens:
            self._wait('sp', tok, waits)
        self.lists['sp'].append((waits, None, None, 0))

    def emit(self):
        nc = self.nc

        def run(lst, eng):
            for waits, fn, sem, inc in lst:
                for s, v in waits:
                    eng.wait_ge(s, v)
                if fn is not None:
                    fn(eng).then_inc(sem, inc)

        with nc.Block() as block:
            @block.tensor
            def _(e):
                run(self.lists['pe'], e)

            @block.scalar
            def _(e):
                run(self.lists['act'], e)

            @block.vector
            def _(e):
                run(self.lists['dve'], e)

            @block.gpsimd
            def _(e):
                run(self.lists['pool'], e)

            @block.sync
            def _(e):
                run(self.lists['sp'], e)


DBG = {}

T = 256
NKB = SEQ // 128
NQB = 1024 // 128
SEGS = dict(qa=(0, 1536), ka=(1536, 512), va=(2048, 512), iq=(2560, 1024), ik=(3584, 64), iw=(3648, 16),
            za=(3664, 1536), qb=(5200, 1536), kb=(6736, 512), vb=(7248, 512), fb=(7760, 12), zb=(7772, 1536),
            qc=(9308, 1024), zc=(10332, 1024), ga=(11356, 4096), gb=(15452, 4096), gc=(19548, 4096))
D_IN = 23644
NEG = -1.0e30
SM_SCALE = 128 ** -0.5
SM_SCALE_C = 256 ** -0.5
TOPK = 256
N_BISECT = 22
N_POOL = 2560
N_BISECT_S = 30


def OP(P, eng, method, reads, writes, *args, **kw):
    P.op(eng, lambda e: getattr(e, method)(*args, **kw), reads=reads, writes=writes)


def build_nc():
    from contextlib import ExitStack
    nc = bass.Bass("TRN2", target_bir_lowering=False)
    di = lambda name, shape: nc.dram_tensor(name, shape, F32, kind="ExternalInput").ap()
    do = lambda name, shape: nc.dram_tensor(name, shape, F32, kind="ExternalOutput").ap()
    xk = di("xk", [SEQ, D_MODEL])
    xq = di("xq", [1024, D_MODEL])
    xsm = di("xsm", [128, D_MODEL])
    xmem = di("xmem", [N_MEM, D_MODEL])
    w_in = di("w_in", [D_MODEL, D_IN])
    w_bra = di("w_bra", [1536, D_MODEL])
    w_brb = di("w_brb", [1536, D_MODEL])
    w_brc = di("w_brc", [1024, D_MODEL])
    w_o = di("w_o", [D_MODEL, D_MODEL])
    w_mem = di("w_mem", [D_MODEL, 2048])
    gcols = di("gcols", [128, 2 * KC])
    gfin = di("gfin", [128, D_MODEL])
    bfrow = di("bfrow", [128, 12])
    rotk = di("rotk", [128, (NKB + 1) * 48])
    rotq = di("rotq", [128, NQB * 48])
    cst = di("cst", [128, 512])
    tabs = di("tabs", [128, 8 + 16 + 8])
    qrow_d = di("qrow", [128, 1024])
    kposrow_d = di("kposrow", [128, SEQ])
    selbc_d = di("selbc", [128, NQB * NKB * 12])
    if DBG.get('sample', True):
        NPOOL_ROWS = N_POOL * 128
        cak = di("cak", [NPOOL_ROWS, 512])
        cav = di("cav", [NPOOL_ROWS, 512])
        cbk = di("cbk", [NPOOL_ROWS, 512])
        cbv = di("cbv", [NPOOL_ROWS, 512])
        cai = di("cai", [NPOOL_ROWS, 64])
        cbl = di("cbl", [NPOOL_ROWS, 12])
        cmk = di("cmk", [SAMP_PER_CORE * N_MEM, 1024])
        cmv = di("cmv", [SAMP_PER_CORE * N_MEM, 1024])
        ptb_d = nc.dram_tensor("ptb", [128, SAMP_PER_CORE * 64], mybir.dt.int32, kind="ExternalInput").ap()
        smc_d = di("smc", [128, 4 * 128 + 128 + 8])
    oys = do("oys", [128, D_MODEL])
    okv = do("okv", [SEQ, KV_COLS])
    osamp = do("osamp", [128, KV_COLS])
    omem = do("omem", [N_MEM, 2048])
    oy = do("oy", [1024, D_MODEL])
    odbg = do("odbg", [4096, T]) if DBG.get('dump') else None

    with ExitStack() as st:
        P = Prog(nc, st)
        sb = lambda name, shape, d=F32: st.enter_context(nc.sbuf_tensor(name, shape, d))
        ps = lambda name, shape, d=F32: st.enter_context(nc.psum_tensor(name, shape, d))

        kaT = sb("kaT", [128, 4, SEQ], BF16)
        kbT = sb("kbT", [128, 4, SEQ], BF16)
        va = sb("va", [128, NKB, 4, 130], BF16)
        vb = sb("vb", [128, NKB, 4, 130], BF16)
        ik2T = sb("ik2T", [128, SEQ], BF16)
        negc = sb("negc", [128, NKB, 12])
        cprev = sb("cprev", [128, 12])
        mkT = sb("mkT", [128, 8, N_MEM], BF16)
        mv = sb("mv", [128, 2, 4, 258], BF16)
        cs = sb("cs", [128, 512])
        ident_b = sb("ident_b", [128, 128], BF16)
        gc = sb("gc", [128, 2 * KC])
        bfr = sb("bfr", [128, 12])
        rk = sb("rk", [128, (NKB + 1) * 48])
        rq = sb("rq", [128, NQB * 48])
        tb_ = sb("tabs_sb", [128, 32])
        U_, E127_, E64_ = cs[:, 128:256], cs[:, 256:384], cs[:, 384:512]
        kq_, kpos_, qpos_ = tb_[:, 0:8], tb_[:, 8:24], tb_[:, 24:32]
        xs = sb("xs", [128, D_MODEL])
        xnT = sb("xnT", [128, KC, T], BF16)
        wst = [sb("wst%d" % i, [128, 8, WG]) for i in range(3)]
        wbf = [sb("wbf%d" % i, [128, KC, WG], BF16) for i in range(2)]
        ost = [sb("ost%d" % i, [128, T]) for i in range(3)]
        ss = sb("ss", [128, 4])
        tmp = sb("tmp", [128, 8, 16])
        lf = sb("lf", [128, 3, 12])
        kbf = sb("kbf", [128, 128], BF16)
        ozT = sb("ozT", [128, 32, T], BF16)
        qT = sb("qT", [128, 12, T], BF16)
        iqT = sb("iqT", [128, 8, T], BF16)
        iw = sb("iw", [128, 2, 16])
        R16 = sb("R16", [128, 4096])
        I_ = R16[:, 0:2048]
        xb = R16[:, 0:2048].bitcast(BF16)
        msk = R16[:, 2048:4096]
        M_ = R16[:, 2048:3072].bitcast(BF16)
        MT = R16[:, 3072:4096].bitcast(BF16)
        hT = R16[:, :].bitcast(BF16)
        RKEYS = ['I', 'M', 'MT']
        PT = [sb("PT%d" % i, [128, 384], BF16) for i in range(2)]
        osb = sb("osb", [128, 392])
        rcp = sb("rcp", [128, 4])
        onb = sb("onb", [128, 384], BF16)
        bis = sb("bis", [128, 8])
        fbias = sb("fbias", [128, NKB, 12])
        tmpc = sb("tmpc", [128, NKB, 12])
        selb = sb("selb", [128, NKB * 12])
        cbn = sb("cbn", [128, 12])
        qrow = sb("qrow_sb", [128, T])
        zs = sb("zs", [128, T], BF16)
        sg = sb("sg", [128, T])
        brs = sb("brs", [128, 3, T])
        hacc = sb("hacc", [128, T])
        gpc = [sb("gpc%d" % i, [128, 512]) for i in range(2)]
        rl = gpc

        pt = [ps("pt%d" % i, [128, 1024], BF16) for i in range(2)]
        pm = [ps("pm%d" % i, [128, 512]) for i in range(4)]
        pa = [ps("pa%d" % i, [128, 512]) for i in range(2)]

        cnt = dict(pmS=0, pm=0, pt=0, pa=0, w=0, stg=0, ost=0, cast=0, PT=0, rl=0, gp=0)

        def nxt(k, n):
            v = cnt[k] % n
            cnt[k] += 1
            return v

        P.dma('sp', cs[:, :], cst, writes=['cs'])
        P.dma('sp', gc[:, :], gcols, writes=['gc'])
        P.dma('sp', bfr[:, :], bfrow, writes=['bfr'])
        P.dma('sp', rk[:, :], rotk, writes=['rk'])
        P.dma('sp', rq[:, :], rotq, writes=['rq'])
        P.dma('sp', tb_[:, :], tabs, writes=['tabs'])
        OP(P, 'dve', 'tensor_copy', ['cs'], ['ident_b'], out=ident_b[:, :], in_=cs[:, 0:128])
        OP(P, 'dve', 'memset', [], ['va'], va[:, :, :, 128:129], 1.0)
        OP(P, 'dve', 'memset', [], ['vb'], vb[:, :, :, 128:129], 1.0)
        OP(P, 'dve', 'memset', [], ['mv'], mv[:, :, :, 256:257], 1.0)
        OP(P, 'dve', 'memset', [], ['cprev'], cprev[:, :], 0.0)

        def norm_block(src_rows, col0, goff):
            P.dma('sp', xs[:, :], src_rows, writes=['xs'])
            OP(P, 'dve', 'memset', [], ['ss'], ss[:, :], 0.0)
            OP(P, 'act', 'activation', ['xs'], ['I', 'ss'], out=xb, in_=xs[:, :], func=AF.Square, accum_out=ss[:, 0:1])
            OP(P, 'dve', 'tensor_scalar', ['ss'], ['ss'], out=ss[:, 1:2], in0=ss[:, 0:1], scalar1=1.0 / D_MODEL,
               scalar2=EPS, op0=ALU.mult, op1=ALU.add)
            OP(P, 'act', 'activation', ['ss'], ['ss'], out=ss[:, 3:4], in_=ss[:, 1:2], func=AF.Sqrt)
            OP(P, 'dve', 'reciprocal', ['ss'], ['ss'], out=ss[:, 2:3], in_=ss[:, 3:4])
            OP(P, 'dve', 'tensor_scalar', ['xs', 'ss'], ['I'], out=xb, in0=xs[:, :], scalar1=ss[:, 2:3], scalar2=None,
               op0=ALU.mult)
            for g4 in range(4):
                pb = nxt('pt', 2)
                for j in range(8):
                    kc = g4 * 8 + j
                    OP(P, 'pe', 'transpose', ['I', 'ident_b'], ['pt%d' % pb], out=pt[pb][:, j * 128:(j + 1) * 128],
                       in_=xb[:, kc * 128:(kc + 1) * 128], identity=ident_b[:, :])
                for j in range(8):
                    kc = g4 * 8 + j
                    OP(P, 'dve', 'tensor_scalar', ['pt%d' % pb, 'gc'], ['xnT'], out=xnT[:, kc, col0:col0 + 128],
                       in0=pt[pb][:, j * 128:(j + 1) * 128], scalar1=gc[:, goff + kc:goff + kc + 1], scalar2=None,
                       op0=ALU.mult)

        def wload(parts, K):
            wi = nxt('w', 2)
            for (w2d, c0, ncol, dcol) in parts:
                wv = w2d.rearrange("(kc p) c -> p kc c", p=128)
                for k0 in range(0, K, 8):
                    kn = min(8, K - k0)
                    sg_ = nxt('stg', 3)
                    P.dma('sp', wst[sg_][:, 0:kn, 0:ncol], wv[:, k0:k0 + kn, c0:c0 + ncol], writes=['wst%d' % sg_])
                    eng = ('act', 'dve', 'act', 'pool')[nxt('cast', 4)]
                    if eng == 'act':
                        OP(P, 'act', 'copy', ['wst%d' % sg_], ['wbf%d_%d' % (wi, k0 // 8)],
                           out=wbf[wi][:, k0:k0 + kn, dcol:dcol + ncol], in_=wst[sg_][:, 0:kn, 0:ncol])
                    else:
                        OP(P, eng, 'tensor_copy', ['wst%d' % sg_], ['wbf%d_%d' % (wi, k0 // 8)],
                           out=wbf[wi][:, k0:k0 + kn, dcol:dcol + ncol], in_=wst[sg_][:, 0:kn, 0:ncol])
            return wi

        def mm_tm(wi, tok0, ncol, K=KC):
            pb = nxt('pm', 4)
            for kc in range(K):
                OP(P, 'pe', 'matmul', ['xnT', 'wbf%d_%d' % (wi, kc // 8)], ['pm%d' % pb], pm[pb][:, 0:ncol],
                   lhsT=xnT[:, kc, tok0:tok0 + 128], rhs=wbf[wi][:, kc, 0:ncol], start=(kc == 0), stop=(kc == K - 1))
            return pb

        def mm_fm(wi, ncol, rhsT, rkey, K, koff=0, ntok=T, pool='pm'):
            if pool == 'pm':
                pb = nxt('pm', 4)
                dst, dkey = pm[pb], 'pm%d' % pb
            else:
                pb = nxt('pa', 2)
                dst, dkey = pa[pb], 'pa%d' % pb
            for kc in range(K):
                OP(P, 'pe', 'matmul', [rkey, 'wbf%d_%d' % (wi, kc // 8)], [dkey], dst[0:ncol, 0:ntok],
                   lhsT=wbf[wi][:, kc, 0:ncol], rhs=rhsT[:, koff + kc, 0:ntok], start=(kc == 0), stop=(kc == K - 1))
            return dst, dkey

        def evac_tm(pb, ncol):
            o = nxt('ost', 3)
            OP(P, 'act', 'copy', ['pm%d' % pb], ['ost%d' % o], out=ost[o][:, 0:ncol], in_=pm[pb][:, 0:ncol])
            return o

        def rot(o, off, half, cos, sin, rkey):
            okey = 'ost%d' % o
            x1 = ost[o][:, off:off + half]
            x2 = ost[o][:, off + half:off + 2 * half]
            t = lambda i: tmp[:, i, 0:half]
            OP(P, 'dve', 'tensor_tensor', [okey, rkey], ['tmp'], out=t(0), in0=x1, in1=cos, op=ALU.mult)
            OP(P, 'dve', 'tensor_tensor', [okey, rkey], ['tmp'], out=t(1), in0=x2, in1=sin, op=ALU.mult)
            OP(P, 'dve', 'tensor_tensor', [okey, rkey], ['tmp'], out=t(2), in0=x1, in1=sin, op=ALU.mult)
            OP(P, 'dve', 'tensor_tensor', [okey, rkey], ['tmp'], out=t(3), in0=x2, in1=cos, op=ALU.mult)
            OP(P, 'dve', 'tensor_tensor', ['tmp'], [okey], out=x1, in0=t(0), in1=t(1), op=ALU.subtract)
            OP(P, 'dve', 'tensor_tensor', ['tmp'], [okey], out=x2, in0=t(2), in1=t(3), op=ALU.add)

        def transpose_out(src_bf, skey, nrow_out, dst, dkey, eng='act'):
            c = cnt['pt']
            cnt['pt'] += 1
            bank, slot = (c // 8) % 2, c % 8
            OP(P, 'pe', 'transpose', [skey, 'ident_b'], ['pt%d' % bank], out=pt[bank][0:nrow_out, slot * 128:(slot + 1) * 128],
               in_=src_bf, identity=ident_b[:, :])
            if eng == 'act':
                OP(P, 'act', 'copy', ['pt%d' % bank], [dkey], out=dst, in_=pt[bank][0:nrow_out, slot * 128:(slot + 1) * 128])
            else:
                OP(P, 'dve', 'tensor_copy', ['pt%d' % bank], [dkey], out=dst, in_=pt[bank][0:nrow_out, slot * 128:(slot + 1) * 128])

        def logf_chain(o, fo):
            okey = 'ost%d' % o
            OP(P, 'dve', 'tensor_tensor', [okey, 'bfr'], ['lf'], out=lf[:, 0, :], in0=ost[o][:, fo:fo + 12], in1=bfr[:, :],
               op=ALU.add)
            OP(P, 'act', 'activation', ['lf'], ['lf'], out=lf[:, 1, :], in_=lf[:, 0, :], func=AF.Exp, scale=-1.0)
            OP(P, 'dve', 'tensor_scalar', ['lf'], ['lf'], out=lf[:, 2, :], in0=lf[:, 1, :], scalar1=1.0, scalar2=None,
               op0=ALU.add)
            OP(P, 'act', 'activation', ['lf'], ['lf'], out=lf[:, 1, :], in_=lf[:, 2, :], func=AF.Ln)
            OP(P, 'dve', 'tensor_scalar', ['lf'], [okey], out=ost[o][:, fo:fo + 12], in0=lf[:, 1, :], scalar1=-1.0,
               scalar2=None, op0=ALU.mult)

        kv_groups = ([('ka', h) for h in range(4)] + [('va', h) for h in range(4)] + [('kb', h) for h in range(4)]
                     + [('vb', h) for h in range(4)] + [('ikfb', 0)])
        OUTC = dict(ka=C_KA, va=C_VA, kb=C_KB, vb=C_VB, ikfb=C_IK)

        def kv_tile(rows_ap, nblk, kb0, out_ap, rtab, rtab_key, rtab_blk0, resident, smp=None):
            for (kind, h) in kv_groups:
                if kind == 'ikfb':
                    parts = [(w_in, SEGS['ik'][0], 64, 0), (w_in, SEGS['fb'][0], 12, 64)]
                    ncol = 76
                else:
                    parts = [(w_in, SEGS[kind][0] + h * 128, 128, 0)]
                    ncol = 128
                wi = wload(parts, KC)
                for b in range(nblk):
                    kb = kb0 + b
                    pb = mm_tm(wi, b * 128, ncol)
                    o = evac_tm(pb, ncol)
                    okey = 'ost%d' % o
                    rb = (rtab_blk0 + b) * 48
                    if kind == 'ka':
                        rot(o, 0, 16, rtab[:, rb:rb + 16], rtab[:, rb + 16:rb + 32], rtab_key)
                    if kind == 'ikfb':
                        rot(o, 0, 8, rtab[:, rb + 32:rb + 40], rtab[:, rb + 40:rb + 48], rtab_key)
                        logf_chain(o, 64)
                    c0 = OUTC[kind] + (h * 128 if kind != 'ikfb' else 0)
                    P.dma('act', out_ap[b * 128:(b + 1) * 128, c0:c0 + ncol], ost[o][:, 0:ncol], reads=[okey], is_output=True)
                    if smp is not None:
                        if kind in ('ka', 'kb'):
                            OP(P, 'dve', 'tensor_copy', [okey], ['kbf'], out=kbf[:, :], in_=ost[o][:, 0:128])
                            transpose_out(kbf[:, :], 'kbf', 128, smp[kind][:, h * 128:(h + 1) * 128], 's' + kind)
                        elif kind in ('va', 'vb'):
                            OP(P, 'dve', 'tensor_copy', [okey], ['s' + kind], out=smp[kind][:, h * 130:h * 130 + 128], in_=ost[o][:, 0:128])
                        else:
                            OP(P, 'dve', 'tensor_copy', [okey], ['kbf'], out=kbf[:, 0:64], in_=ost[o][:, 0:64])
                            OP(P, 'dve', 'tensor_copy', [okey], ['kbf'], out=kbf[:, 64:128], in_=ost[o][:, 0:64])
                            transpose_out(kbf[:, :], 'kbf', 128, smp['ik'][:, :], 'sik')
                            OP(P, 'dve', 'tensor_copy', [okey], ['slogf'], out=smp['logf'], in_=ost[o][:, 64:76])
                        continue
                    if not resident:
                        continue
                    if kind in ('ka', 'kb'):
                        OP(P, 'dve', 'tensor_copy', [okey], ['kbf'], out=kbf[:, :], in_=ost[o][:, 0:128])
                        dstT = kaT if kind == 'ka' else kbT
                        transpose_out(kbf[:, :], 'kbf', 128, dstT[:, h, kb * 128:(kb + 1) * 128], kind + 'T')
                    elif kind in ('va', 'vb'):
                        dv = va if kind == 'va' else vb
                        OP(P, 'dve', 'tensor_copy', [okey], [kind], out=dv[:, kb, h, 0:128], in_=ost[o][:, 0:128])
                    else:
                        OP(P, 'dve', 'tensor_copy', [okey], ['kbf'], out=kbf[:, 0:64], in_=ost[o][:, 0:64])
                        OP(P, 'dve', 'tensor_copy', [okey], ['kbf'], out=kbf[:, 64:128], in_=ost[o][:, 0:64])
                        transpose_out(kbf[:, :], 'kbf', 128, ik2T[:, kb * 128:(kb + 1) * 128], 'ik2T')
                        pb2 = nxt('pa', 2)
                        OP(P, 'pe', 'matmul', [okey, 'cs'], ['pa%d' % pb2], pa[pb2][:, 0:12], lhsT=U_, rhs=ost[o][:, 64:76],
                           start=True, stop=False)
                        OP(P, 'pe', 'matmul', ['cprev', 'cs'], ['pa%d' % pb2], pa[pb2][:, 0:12], lhsT=E127_, rhs=cprev[:, :],
                           start=False, stop=True)
                        OP(P, 'act', 'copy', ['pa%d' % pb2], ['cprev'], out=cprev[:, :], in_=pa[pb2][:, 0:12])
                        OP(P, 'dve', 'tensor_scalar', ['cprev'], ['negc'], out=negc[:, kb, :], in0=cprev[:, :], scalar1=-1.0,
                           scalar2=None, op0=ALU.mult)

        n_kt = DBG.get('n_kt', SEQ // T)
        for kt in range(n_kt):
            for b in range(2):
                norm_block(xk[(kt * 2 + b) * 128:(kt * 2 + b + 1) * 128, :], b * 128, 0)
            kv_tile(None, 2, kt * 2, okv[kt * T:(kt + 1) * T, :], rk, 'rk', kt * 2, True)
        if DBG.get('mem', True):
            for b in range(2):
                norm_block(xmem[b * 128:(b + 1) * 128, :], b * 128, KC)
            for g in range(16):
                wi = wload([(w_mem, g * 128, 128, 0)], KC)
                for b in range(2):
                    pb = mm_tm(wi, b * 128, 128)
                    o = evac_tm(pb, 128)
                    okey = 'ost%d' % o
                    P.dma('act', omem[b * 128:(b + 1) * 128, g * 128:(g + 1) * 128], ost[o][:, 0:128], reads=[okey],
                          is_output=True)
                    if g < 8:
                        OP(P, 'dve', 'tensor_copy', [okey], ['kbf'], out=kbf[:, :], in_=ost[o][:, 0:128])
                        transpose_out(kbf[:, :], 'kbf', 128, mkT[:, g, b * 128:(b + 1) * 128], 'mkT')
                    else:
                        hh, dc = (g - 8) // 2, (g - 8) % 2
                        OP(P, 'dve', 'tensor_copy', [okey], ['mv'], out=mv[:, b, hh, dc * 128:(dc + 1) * 128],
                           in_=ost[o][:, 0:128])

        def softmax_out(accs, nh, dhead, ch0, blk):
            stride = dhead + 2
            for g in range(nh):
                at, ak = accs[g]
                OP(P, 'act', 'copy', [ak], ['osb'], out=osb[:, g * stride:g * stride + dhead + 1], in_=at[:, 0:dhead + 1])
            for g in range(nh):
                OP(P, 'dve', 'reciprocal', ['osb'], ['rcp'], out=rcp[:, g:g + 1],
                   in_=osb[:, g * stride + dhead:g * stride + dhead + 1])
                OP(P, 'dve', 'tensor_scalar', ['osb', 'rcp'], ['onb'], out=onb[:, g * dhead:(g + 1) * dhead],
                   in0=osb[:, g * stride:g * stride + dhead], scalar1=rcp[:, g:g + 1], scalar2=None, op0=ALU.mult)
            for c in range(nh * dhead // 128):
                transpose_out(onb[:, c * 128:(c + 1) * 128], 'onb', 128, ozT[:, ch0 + c, blk * 128:(blk + 1) * 128], 'ozT')

        def attend_ab(b, kT, kkey, vv, vkey, ch0, fb, nkb=NKB):
            accs = [(pa[0], 'pa0'), (pa[1], 'pa1'), (pm[3], 'pm3')]
            for n in range(4):
                for i in range(nkb):
                    pb = nxt('pmS', 3)
                    for g in range(3):
                        OP(P, 'pe', 'matmul', [kkey, 'qT'], ['pm%d' % pb], pm[pb][:, g * 128:(g + 1) * 128],
                           lhsT=kT[:, n, i * 128:(i + 1) * 128], rhs=qT[:, 3 * n + g, b * 128:(b + 1) * 128], start=True, stop=True)
                    y = nxt('PT', 2)
                    if fb is None:
                        OP(P, 'act', 'activation', ['pm%d' % pb], ['PT%d' % y], out=PT[y][:, 0:384], in_=pm[pb][:, 0:384],
                           func=AF.Exp, scale=SM_SCALE)
                    else:
                        for g in range(3):
                            OP(P, 'act', 'activation', ['pm%d' % pb, 'fbias'], ['PT%d' % y], out=PT[y][:, g * 128:(g + 1) * 128],
                               in_=pm[pb][:, g * 128:(g + 1) * 128], func=AF.Exp, scale=SM_SCALE,
                               bias=fb[:, i, 3 * n + g:3 * n + g + 1])
                    for g in range(3):
                        OP(P, 'pool', 'tensor_tensor', ['PT%d' % y, 'MT'], ['PT%d' % y], out=PT[y][:, g * 128:(g + 1) * 128],
                           in0=PT[y][:, g * 128:(g + 1) * 128], in1=MT[:, i * 128:(i + 1) * 128], op=ALU.mult)
                    for g in range(3):
                        at, ak = accs[g]
                        OP(P, 'pe', 'matmul', ['PT%d' % y, vkey], [ak], at[:, 0:129],
                           lhsT=PT[y][:, g * 128:(g + 1) * 128], rhs=vv[:, i, n, 0:129], start=(i == 0), stop=(i == nkb - 1))
                softmax_out(accs, 3, 128, ch0 + 3 * n, b)

        def gate_z(seg, ch0, nch):
            for c in range(nch):
                wi = wload([(w_in, SEGS[seg][0] + c * 128, 128, 0)], KC)
                dst, dkey = mm_fm(wi, 128, xnT, 'xnT', KC)
                OP(P, 'act', 'activation', [dkey], ['zs'], out=zs[:, :], in_=dst[:, 0:T], func=AF.Silu)
                OP(P, 'pool', 'tensor_tensor', ['zs', 'ozT'], ['ozT'], out=ozT[:, ch0 + c, :], in0=ozT[:, ch0 + c, :],
                   in1=zs[:, :], op=ALU.mult)

        def q_fm(seg, nch):
            for c in range(nch):
                wi = wload([(w_in, SEGS[seg][0] + c * 128, 128, 0)], KC)
                dst, dkey = mm_fm(wi, 128, xnT, 'xnT', KC)
                OP(P, 'act', 'copy', [dkey], ['qT'], out=qT[:, c, :], in_=dst[:, 0:T])

        def phase_D():
            for fc in range(32):
                wa = wload([(w_bra, fc * 128, 128, 0)], 12)
                dA, kA = mm_fm(wa, 128, ozT, 'ozT', 12, koff=0)
                OP(P, 'act', 'copy', [kA], ['brs'], out=brs[:, 0, :], in_=dA[:, 0:T])
                wb_ = wload([(w_brb, fc * 128, 128, 0)], 12)
                dB, kB = mm_fm(wb_, 128, ozT, 'ozT', 12, koff=12)
                OP(P, 'act', 'copy', [kB], ['brs'], out=brs[:, 1, :], in_=dB[:, 0:T])
                wc_ = wload([(w_brc, fc * 128, 128, 0)], 8)
                dC, kC = mm_fm(wc_, 128, ozT, 'ozT', 8, koff=24)
                OP(P, 'act', 'copy', [kC], ['brs'], out=brs[:, 2, :], in_=dC[:, 0:T])
                for bi, seg in enumerate(('ga', 'gb', 'gc')):
                    wg_ = wload([(w_in, SEGS[seg][0] + fc * 128, 128, 0)], KC)
                    dG, kG = mm_fm(wg_, 128, xnT, 'xnT', KC)
                    OP(P, 'act', 'activation', [kG], ['sg'], out=sg[:, :], in_=dG[:, 0:T], func=AF.Sigmoid)
                    if bi == 0:
                        OP(P, 'dve', 'tensor_tensor', ['sg', 'brs'], ['hacc'], out=hacc[:, :], in0=sg[:, :], in1=brs[:, 0, :],
                           op=ALU.mult)
                    else:
                        OP(P, 'dve', 'tensor_tensor', ['sg', 'brs'], ['sg'], out=sg[:, :], in0=sg[:, :], in1=brs[:, bi, :],
                           op=ALU.mult)
                        OP(P, 'dve', 'tensor_tensor', ['sg', 'hacc'], ['hacc'], out=hacc[:, :], in0=hacc[:, :], in1=sg[:, :],
                           op=ALU.add)
                OP(P, 'dve', 'tensor_copy', ['hacc'], RKEYS, out=hT[:, fc * T:(fc + 1) * T], in_=hacc[:, :])

        def phase_E(b, x_rows, y_rows):
            P.dma('sp', xs[:, :], x_rows, writes=['xs'])
            for cg in range(32):
                wo = wload([(w_o, cg * 128, 128, 0)], KC)
                pb = nxt('pm', 4)
                for kc in range(KC):
                    OP(P, 'pe', 'matmul', RKEYS + ['wbf%d_%d' % (wo, kc // 8)], ['pm%d' % pb], pm[pb][:, 0:128],
                       lhsT=hT[:, kc * T + b * 128:kc * T + (b + 1) * 128], rhs=wbf[wo][:, kc, 0:128],
                       start=(kc == 0), stop=(kc == KC - 1))
                o = evac_tm(pb, 128)
                OP(P, 'dve', 'tensor_tensor', ['ost%d' % o, 'xs'], ['xs'], out=xs[:, cg * 128:(cg + 1) * 128],
                   in0=xs[:, cg * 128:(cg + 1) * 128], in1=ost[o][:, 0:128], op=ALU.add)
            OP(P, 'dve', 'memset', [], ['ss'], ss[:, :], 0.0)
            for pc in range(8):
                gi = nxt('gp', 2)
                OP(P, 'dve', 'tensor_tensor', ['xs'], ['gpc%d' % gi], out=gpc[gi][:, :], in0=xs[:, pc * 512:(pc + 1) * 512],
                   in1=xs[:, pc * 512:(pc + 1) * 512], op=ALU.mult)
                OP(P, 'dve', 'tensor_reduce', ['gpc%d' % gi], ['rcp'], out=rcp[:, 0:1], in_=gpc[gi][:, :],
                   axis=mybir.AxisListType.X, op=ALU.add)
                OP(P, 'dve', 'tensor_tensor', ['rcp', 'ss'], ['ss'], out=ss[:, 0:1], in0=ss[:, 0:1], in1=rcp[:, 0:1],
                   op=ALU.add)
            OP(P, 'dve', 'tensor_scalar', ['ss'], ['ss'], out=ss[:, 1:2], in0=ss[:, 0:1], scalar1=1.0 / D_MODEL,
               scalar2=EPS, op0=ALU.mult, op1=ALU.add)
            OP(P, 'act', 'activation', ['ss'], ['ss'], out=ss[:, 3:4], in_=ss[:, 1:2], func=AF.Sqrt)
            OP(P, 'dve', 'reciprocal', ['ss'], ['ss'], out=ss[:, 2:3], in_=ss[:, 3:4])
            OP(P, 'dve', 'tensor_scalar', ['xs', 'ss'], ['xs'], out=xs[:, :], in0=xs[:, :], scalar1=ss[:, 2:3],
               scalar2=None, op0=ALU.mult)
            for pc in range(8):
                gi = nxt('gp', 2)
                P.dma('sp', gpc[gi][:, :], gfin[:, pc * 512:(pc + 1) * 512], writes=['gpc%d' % gi])
                OP(P, 'dve', 'tensor_tensor', ['gpc%d' % gi, 'xs'], ['xs'], out=xs[:, pc * 512:(pc + 1) * 512],
                   in0=xs[:, pc * 512:(pc + 1) * 512], in1=gpc[gi][:, :], op=ALU.mult)
            P.dma('act', y_rows, xs[:, :], reads=['xs'], is_output=True)


        n_qt = DBG.get('n_qt', 1024 // T)
        phases = DBG.get('phases', 'ABCDE')
        for qt in range(n_qt):
            for b in range(2):
                norm_block(xq[(qt * 2 + b) * 128:(qt * 2 + b + 1) * 128, :], b * 128, 0)
            P.dma('sp', qrow[:, :], qrow_d[:, qt * T:(qt + 1) * T], writes=['qrow'])
            if 'A' in phases:
                for h in range(12):
                    wi = wload([(w_in, SEGS['qa'][0] + h * 128, 128, 0)], KC)
                    for b in range(2):
                        jb = qt * 2 + b
                        pb = mm_tm(wi, b * 128, 128)
                        o = evac_tm(pb, 128)
                        rot(o, 0, 16, rq[:, jb * 48:jb * 48 + 16], rq[:, jb * 48 + 16:jb * 48 + 32], 'rq')
                        OP(P, 'dve', 'tensor_copy', ['ost%d' % o], ['kbf'], out=kbf[:, :], in_=ost[o][:, 0:128])
                        transpose_out(kbf[:, :], 'kbf', 128, qT[:, h, b * 128:(b + 1) * 128], 'qT')
                for g in range(8):
                    wi = wload([(w_in, SEGS['iq'][0] + g * 128, 128, 0)], KC)
                    for b in range(2):
                        jb = qt * 2 + b
                        pb = mm_tm(wi, b * 128, 128)
                        o = evac_tm(pb, 128)
                        for hh in range(2):
                            rot(o, hh * 64, 8, rq[:, jb * 48 + 32:jb * 48 + 40], rq[:, jb * 48 + 40:jb * 48 + 48], 'rq')
                        OP(P, 'dve', 'tensor_copy', ['ost%d' % o], ['kbf'], out=kbf[:, :], in_=ost[o][:, 0:128])
                        transpose_out(kbf[:, :], 'kbf', 128, iqT[:, g, b * 128:(b + 1) * 128], 'iqT')
                wi = wload([(w_in, SEGS['iw'][0], 16, 0)], KC)
                for b in range(2):
                    pb = mm_tm(wi, b * 128, 16)
                    OP(P, 'act', 'activation', ['pm%d' % pb], ['iw'], out=iw[:, b, :], in_=pm[pb][:, 0:16], func=AF.Copy,
                       scale=(64 ** -0.5) * (16 ** -0.5))
                for b in range(2):
                    jb = qt * 2 + b
                    nkb = min(NKB, 2 * jb + 2)
                    nk = nkb * 128
                    Iv, mskv, Mv = I_[:, 0:nk], msk[:, 0:nk], M_[:, 0:nk]
                    OP(P, 'dve', 'memset', [], ['I'], Iv, 0.0)
                    for h in range(16):
                        g, hh = h // 2, h % 2
                        for q in range((nk + 511) // 512):
                            wq = min(512, nk - q * 512)
                            pb = nxt('pm', 4)
                            OP(P, 'pe', 'matmul', ['iqT', 'ik2T'], ['pm%d' % pb], pm[pb][:, 0:wq],
                               lhsT=iqT[hh * 64:(hh + 1) * 64, g, b * 128:(b + 1) * 128],
                               rhs=ik2T[hh * 64:(hh + 1) * 64, q * 512:q * 512 + wq], start=True, stop=True)
                            r = nxt('rl', 2)
                            OP(P, 'act', 'activation', ['pm%d' % pb], ['gpc%d' % r], out=rl[r][:, 0:wq], in_=pm[pb][:, 0:wq],
                               func=AF.Relu)
                            OP(P, 'dve', 'scalar_tensor_tensor', ['gpc%d' % r, 'iw', 'I'], ['I'], out=I_[:, q * 512:q * 512 + wq],
                               in0=rl[r][:, 0:wq], scalar=iw[:, b, h:h + 1], in1=I_[:, q * 512:q * 512 + wq], op0=ALU.mult,
                               op1=ALU.add)
                    OP(P, 'dve', 'tensor_reduce', ['I'], ['bis'], out=bis[:, 0:1], in_=Iv, axis=mybir.AxisListType.X, op=ALU.min)
                    OP(P, 'dve', 'tensor_reduce', ['I'], ['bis'], out=bis[:, 5:6], in_=Iv, axis=mybir.AxisListType.X, op=ALU.max)
                    P.dma('sp', mskv, kposrow_d[:, 0:nk], writes=['M', 'MT'])
                    OP(P, 'dve', 'tensor_scalar', ['M', 'MT', 'tabs'], ['M', 'MT'], out=mskv, in0=mskv, scalar1=qpos_[:, jb:jb + 1],
                       scalar2=None, op0=ALU.is_gt)
                    OP(P, 'dve', 'scalar_tensor_tensor', ['M', 'MT', 'I'], ['I'], out=Iv, in0=mskv, scalar=NEG, in1=Iv,
                       op0=ALU.mult, op1=ALU.add)
                    OP(P, 'dve', 'tensor_tensor', ['bis'], ['bis'], out=bis[:, 1:2], in0=bis[:, 5:6], in1=bis[:, 0:1],
                       op=ALU.subtract)
                    OP(P, 'dve', 'tensor_scalar', ['bis'], ['bis'], out=bis[:, 1:2], in0=bis[:, 1:2], scalar1=1.0001,
                       scalar2=1e-6, op0=ALU.mult, op1=ALU.add)
                    for it in range(N_BISECT):
                        OP(P, 'dve', 'tensor_scalar', ['bis'], ['bis'], out=bis[:, 1:2], in0=bis[:, 1:2], scalar1=0.5,
                           scalar2=None, op0=ALU.mult)
                        OP(P, 'dve', 'tensor_tensor', ['bis'], ['bis'], out=bis[:, 2:3], in0=bis[:, 0:1], in1=bis[:, 1:2],
                           op=ALU.add)
                        OP(P, 'dve', 'tensor_scalar', ['I', 'bis', 'M'], ['M', 'bis'], out=Mv, in0=Iv, scalar1=bis[:, 2:3],
                           scalar2=0.0, op0=ALU.is_ge, op1=ALU.add, accum_out=bis[:, 3:4])
                        OP(P, 'dve', 'tensor_tensor', ['bis', 'tabs'], ['bis'], out=bis[:, 4:5], in0=bis[:, 3:4],
                           in1=kq_[:, jb:jb + 1], op=ALU.is_ge)
                        OP(P, 'dve', 'scalar_tensor_tensor', ['bis'], ['bis'], out=bis[:, 0:1], in0=bis[:, 1:2],
                           scalar=bis[:, 4:5], in1=bis[:, 0:1], op0=ALU.mult, op1=ALU.add)
                    OP(P, 'dve', 'tensor_scalar', ['I', 'bis', 'M'], ['M'], out=Mv, in0=Iv, scalar1=bis[:, 0:1], scalar2=None,
                       op0=ALU.is_ge)
                    for i in range(nkb):
                        transpose_out(M_[:, i * 128:(i + 1) * 128], 'M', 128, MT[:, i * 128:(i + 1) * 128], 'MT', eng='dve')
                    attend_ab(b, kaT, 'kaT', va, 'va', 0, None, nkb)
                gate_z('za', 0, 12)
            if 'B' in phases:
                q_fm('qb', 12)
                for b in range(2):
                    jb = qt * 2 + b
                    P.dma('sp', selb[:, :], selbc_d[:, jb * NKB * 12:(jb + 1) * NKB * 12], writes=['selb'])
                    OP(P, 'dve', 'tensor_tensor', ['negc', 'selb'], ['tmpc'], out=tmpc[:, :, :],
                       in0=negc[:, :, :], in1=selb[:, :].rearrange("p (i h) -> p i h", h=12), op=ALU.mult)
                    pb2 = nxt('pa', 2)
                    for i in range(NKB):
                        OP(P, 'pe', 'matmul', ['tmpc', 'cs'], ['pa%d' % pb2], pa[pb2][:, 0:12], lhsT=E64_, rhs=tmpc[:, i, :],
                           start=(i == 0), stop=(i == NKB - 1))
                    OP(P, 'act', 'copy', ['pa%d' % pb2], ['cbn'], out=cbn[:, :], in_=pa[pb2][:, 0:12])
                    for i in range(NKB):
                        OP(P, 'dve', 'tensor_tensor', ['negc', 'cbn'], ['fbias'], out=fbias[:, i, :], in0=negc[:, i, :],
                           in1=cbn[:, :], op=ALU.subtract)
                    OP(P, 'dve', 'tensor_scalar', ['fbias'], ['fbias'], out=fbias[:, :, :], in0=fbias[:, :, :], scalar1=45.0,
                       scalar2=None, op0=ALU.min)
                    for i in range(NKB):
                        OP(P, 'dve', 'tensor_scalar', ['qrow', 'tabs'], ['MT'], out=MT[:, i * 128:(i + 1) * 128],
                           in0=qrow[:, b * 128:(b + 1) * 128], scalar1=kpos_[:, i:i + 1], scalar2=None, op0=ALU.is_ge)
                    attend_ab(b, kbT, 'kbT', vb, 'vb', 12, fbias, min(NKB, 2 * jb + 2))
                gate_z('zb', 12, 12)
            if 'C' in phases:
                q_fm('qc', 8)
                for b in range(2):
                    for h in range(4):
                        pb = nxt('pm', 4)
                        for mb in range(2):
                            for dc in range(2):
                                OP(P, 'pe', 'matmul', ['mkT', 'qT'], ['pm%d' % pb], pm[pb][:, mb * 128:(mb + 1) * 128],
                                   lhsT=mkT[:, 2 * h + dc, mb * 128:(mb + 1) * 128], rhs=qT[:, 2 * h + dc, b * 128:(b + 1) * 128],
                                   start=(dc == 0), stop=(dc == 1))
                        y = nxt('PT', 2)
                        OP(P, 'act', 'activation', ['pm%d' % pb], ['PT%d' % y], out=PT[y][:, 0:256], in_=pm[pb][:, 0:256],
                           func=AF.Exp, scale=SM_SCALE_C)
                        pab = nxt('pa', 2)
                        for mb in range(2):
                            OP(P, 'pe', 'matmul', ['PT%d' % y, 'mv'], ['pa%d' % pab], pa[pab][:, 0:257],
                               lhsT=PT[y][:, mb * 128:(mb + 1) * 128], rhs=mv[:, mb, h, 0:257], start=(mb == 0), stop=(mb == 1))
                        softmax_out([(pa[pab], 'pa%d' % pab)], 1, 256, 24 + 2 * h, b)
                gate_z('zc', 24, 8)
            if DBG.get('dump') == 1 and qt == 0:
                for c in range(32):
                    o = nxt('ost', 3)
                    OP(P, 'act', 'copy', ['ozT'], ['ost%d' % o], out=ost[o][:, 0:T], in_=ozT[:, c, :])
                    P.dma('act', odbg[c * 128:(c + 1) * 128, :], ost[o][:, 0:T], reads=['ost%d' % o], is_output=True)
            if 'D' in phases:
                phase_D()
            if 'E' in phases:
                for b in range(2):
                    jb = qt * 2 + b
                    phase_E(b, xq[jb * 128:(jb + 1) * 128, :], oy[jb * 128:(jb + 1) * 128, :])
        if DBG.get('sample', True):
            P.barrier()
            U32 = mybir.dt.uint32
            I32 = mybir.dt.int32
            SA = kaT[:, :, :].rearrange("p a b -> p (a b)").bitcast(F32)
            SB_ = kbT[:, :, :].rearrange("p a b -> p (a b)").bitcast(F32)
            SV = va[:, :, :, :].rearrange("p a b c -> p (a b c)").bitcast(F32)
            SW = vb[:, :, :, :].rearrange("p a b c -> p (a b c)").bitcast(F32)
            Kpg = [SA[:, 0:512], SA[:, 512:1024]]
            Vpg = [SA[:, 1024:1536], SA[:, 1536:2048]]
            Lg = SA[:, 2048:2816]
            Cc = SA[:, 2816:3584]
            Ipg = [SA[:, 3584:3648], SA[:, 3648:3712]]
            fbp = SB_[:, 0:780]
            pref = SB_[:, 780:1560]
            Tt = SB_[:, 1560:2328]
            I_s = SB_[:, 2328:2393]
            sel = SB_[:, 2400:2465]
            rowf = SB_[:, 2472:2536]
            rowi = SB_[:, 2536:2600].bitcast(U32)
            ptf = SB_[:, 2600:2664]
            ptbi = SB_[:, 2664:2920].bitcast(I32)
            kbfp = SB_[:, 2944:3200].bitcast(BF16)
            kTp = [SB_[:, 3200:3456].bitcast(BF16), SB_[:, 3456:3712].bitcast(BF16)]
            vbfp = [SV[:, 0:260].bitcast(BF16), SV[:, 260:520].bitcast(BF16)]
            ikd = SV[:, 520:584].bitcast(BF16)
            ikTp = SV[:, 584:648].bitcast(BF16)
            wbs = SV[:, 648:664]
            Rs = SV[:, 664:680]
            lfb = SV[:, 680:692]
            cT = SV[:, 692:704]
            bsm = SV[:, 704:712]
            Sx = SV[:, 712:724]
            PTs = SV[:, 724:730].bitcast(BF16)
            osm = SV[:, 736:1264]
            onbs = SV[:, 1264:1776].bitcast(BF16)
            tot12 = SV[:, 1776:1788]
            kaTs = SW[:, 0:256].bitcast(BF16)
            kbTs = SW[:, 256:512].bitcast(BF16)
            vas = SW[:, 512:772].bitcast(BF16)
            vbs = SW[:, 772:1032].bitcast(BF16)
            ik2Ts = SW[:, 1032:1096].bitcast(BF16)
            logfs = SW[:, 1096:1108]
            smc = SW[:, 1200:1848]
            ER = lambda r: smc[:, r * 128:(r + 1) * 128]
            ONES = smc[:, 512:640]
            OH = lambda r: smc[:, 640 + r:641 + r]
            OHN = lambda r: smc[:, 644 + r:645 + r]

            P.dma('sp', smc, smc_d, writes=['smc'])
            P.dma('sp', ptbi, ptb_d, writes=['ptb'])
            for v_ in vbfp + [vas, vbs]:
                OP(P, 'dve', 'memset', [], ['vones'], v_.rearrange("p (n c) -> p n c", c=130)[:, :, 128:129], 1.0)
            norm_block(xsm, 0, 0)
            kv_tile(None, 1, 0, osamp, rk, 'rk', NKB, False,
                    smp=dict(ka=kaTs, kb=kbTs, va=vas, vb=vbs, ik=ik2Ts, logf=logfs))
            jbs = NQB
            rqs = lambda a, b_: rk[:, NKB * 48 + a:NKB * 48 + b_]

            def proj_q_tm(seg, nch, dstT, dkey, rots):
                for g in range(nch):
                    wi = wload([(w_in, SEGS[seg][0] + g * 128, 128, 0)], KC)
                    pb = mm_tm(wi, 0, 128)
                    o = evac_tm(pb, 128)
                    for (off, half, ca, sa_) in rots:
                        rot(o, off, half, rqs(*ca), rqs(*sa_), 'rk')
                    OP(P, 'dve', 'tensor_copy', ['ost%d' % o], ['kbf'], out=kbf[:, :], in_=ost[o][:, 0:128])
                    transpose_out(kbf[:, :], 'kbf', 128, dstT[:, g, 0:128], dkey)

            def rowidx(r):
                OP(P, 'dve', 'tensor_copy', ['ptb'], ['ptf'], out=ptf, in_=ptbi[:, r * 64:(r + 1) * 64])
                OP(P, 'dve', 'tensor_scalar', ['ptf', 'tabs'], ['rowf'], out=rowf, in0=ptf, scalar1=128.0, scalar2=kpos_[:, 0:1],
                   op0=ALU.mult, op1=ALU.add)
                OP(P, 'dve', 'tensor_copy', ['rowf'], ['rowi'], out=rowi, in_=rowf)

            def bcast_rows(r):
                pb2 = nxt('pa', 2)
                OP(P, 'pe', 'matmul', ['smc', 'iw'], ['pa%d' % pb2], pa[pb2][:, 0:16], lhsT=ER(r), rhs=iw[:, 0, :], start=True, stop=True)
                OP(P, 'pe', 'matmul', ['smc', 'slogf'], ['pa%d' % pb2], pa[pb2][:, 16:28], lhsT=ER(r), rhs=logfs, start=True, stop=True)
                OP(P, 'act', 'copy', ['pa%d' % pb2], ['wbs'], out=wbs, in_=pa[pb2][:, 0:16])
                OP(P, 'act', 'copy', ['pa%d' % pb2], ['lfb'], out=lfb, in_=pa[pb2][:, 16:28])

            def index_scores(r):
                for pg in range(65):
                    if pg < 64:
                        k = pg % 2
                        P.idma(Ipg[k], cai, rowi[:, pg:pg + 1], reads=['rowi'], writes=['Ipg%d' % k])
                        OP(P, 'dve', 'tensor_copy', ['Ipg%d' % k], ['ikd'], out=ikd[:, 0:64], in_=Ipg[k])
                        OP(P, 'dve', 'tensor_copy', ['Ipg%d' % k], ['ikd'], out=ikd[:, 64:128], in_=Ipg[k])
                        transpose_out(ikd, 'ikd', 128, ikTp, 'ikTp')
                        kt_, kk_ = ikTp, 'ikTp'
                    else:
                        kt_, kk_ = ik2Ts, 'sik'
                    pb = nxt('pmS', 2)
                    for h in range(16):
                        g, hh = h // 2, h % 2
                        OP(P, 'pe', 'matmul', [kk_, 'iqT'], ['pm%d' % pb], pm[pb][:, h:h + 1], lhsT=kt_[hh * 64:(hh + 1) * 64, :],
                           rhs=iqT[hh * 64:(hh + 1) * 64, g, r:r + 1], start=True, stop=True)
                    OP(P, 'act', 'activation', ['pm%d' % pb], ['Rs'], out=Rs, in_=pm[pb][:, 0:16], func=AF.Relu)
                    OP(P, 'dve', 'tensor_tensor', ['Rs', 'wbs'], ['Rs'], out=Rs, in0=Rs, in1=wbs, op=ALU.mult)
                    OP(P, 'dve', 'tensor_reduce', ['Rs'], ['I_s'], out=I_s[:, pg:pg + 1], in_=Rs, axis=mybir.AxisListType.X, op=ALU.add)
                OP(P, 'dve', 'tensor_tensor', ['I_s', 'smc'], ['I_s'], out=I_s[:, 64:65], in0=I_s[:, 64:65], in1=OHN(r), op=ALU.add)
                OP(P, 'dve', 'memset', [], ['bsm'], bsm[:, 0:1], -64.0)
                OP(P, 'dve', 'memset', [], ['bsm'], bsm[:, 1:2], 128.0)
                for it in range(N_BISECT_S):
                    OP(P, 'dve', 'tensor_scalar', ['bsm'], ['bsm'], out=bsm[:, 1:2], in0=bsm[:, 1:2], scalar1=0.5, scalar2=None, op0=ALU.mult)
                    OP(P, 'dve', 'tensor_tensor', ['bsm'], ['bsm'], out=bsm[:, 2:3], in0=bsm[:, 0:1], in1=bsm[:, 1:2], op=ALU.add)
                    OP(P, 'dve', 'tensor_scalar', ['I_s', 'bsm'], ['sel', 'bsm'], out=sel, in0=I_s, scalar1=bsm[:, 2:3], scalar2=0.0,
                       op0=ALU.is_ge, op1=ALU.add, accum_out=bsm[:, 3:4])
                    pb2 = nxt('pa', 2)
                    OP(P, 'pe', 'matmul', ['smc', 'bsm'], ['pa%d' % pb2], pa[pb2][:, 0:1], lhsT=ONES, rhs=bsm[:, 3:4], start=True, stop=True)
                    OP(P, 'act', 'copy', ['pa%d' % pb2], ['bsm2'], out=bsm[:, 5:6], in_=pa[pb2][:, 0:1])
                    OP(P, 'dve', 'tensor_scalar', ['bsm2'], ['bsm'], out=bsm[:, 4:5], in0=bsm[:, 5:6], scalar1=float(TOPK), scalar2=None, op0=ALU.is_ge)
                    OP(P, 'dve', 'scalar_tensor_tensor', ['bsm'], ['bsm'], out=bsm[:, 0:1], in0=bsm[:, 1:2], scalar=bsm[:, 4:5], in1=bsm[:, 0:1],
                       op0=ALU.mult, op1=ALU.add)
                OP(P, 'dve', 'tensor_scalar', ['I_s', 'bsm'], ['sel'], out=sel, in0=I_s, scalar1=bsm[:, 0:1], scalar2=None, op0=ALU.is_ge)

            def fox_bias(r):
                for pg in range(64):
                    P.idma(Lg[:, pg * 12:(pg + 1) * 12], cbl, rowi[:, pg:pg + 1], reads=['rowi'], writes=['Lg%d' % pg])
                for hf in range(2):
                    pb = nxt('pmS', 2)
                    OP(P, 'pe', 'matmul', ['Lg%d' % p_ for p_ in range(hf * 32, hf * 32 + 32)] + ['cs'], ['pm%d' % pb], pm[pb][:, 0:384], lhsT=U_, rhs=Lg[:, hf * 384:(hf + 1) * 384], start=True, stop=True)
                    OP(P, 'act', 'copy', ['pm%d' % pb], ['Cc'], out=Cc[:, hf * 384:(hf + 1) * 384], in_=pm[pb][:, 0:384])
                    pb = nxt('pmS', 2)
                    OP(P, 'pe', 'matmul', ['Lg%d' % p_ for p_ in range(hf * 32, hf * 32 + 32)] + ['smc'], ['pm%d' % pb], pm[pb][:, 0:384], lhsT=ONES, rhs=Lg[:, hf * 384:(hf + 1) * 384], start=True, stop=True)
                    OP(P, 'act', 'copy', ['pm%d' % pb], ['Tt'], out=Tt[:, hf * 384:(hf + 1) * 384], in_=pm[pb][:, 0:384])
                OP(P, 'dve', 'tensor_reduce', ['Tt'], ['tot12'], out=tot12, in_=Tt.rearrange("p (j h) -> p h j", h=12),
                   axis=mybir.AxisListType.X, op=ALU.add)
                OP(P, 'dve', 'tensor_tensor', ['tot12', 'lfb'], ['cT'], out=cT, in0=tot12, in1=lfb, op=ALU.add)
                OP(P, 'dve', 'tensor_scalar', ['cT'], ['pref'], out=pref[:, 0:12], in0=cT, scalar1=-1.0, scalar2=None, op0=ALU.mult)
                for j in range(1, 64):
                    OP(P, 'dve', 'tensor_tensor', ['pref', 'Tt'], ['pref'], out=pref[:, j * 12:(j + 1) * 12], in0=pref[:, (j - 1) * 12:j * 12],
                       in1=Tt[:, (j - 1) * 12:j * 12], op=ALU.add)
                OP(P, 'dve', 'tensor_tensor', ['Cc', 'pref'], ['Cc'], out=Cc, in0=Cc, in1=pref[:, 0:768], op=ALU.add)
                OP(P, 'dve', 'tensor_scalar', ['Cc'], ['fbp'], out=fbp[:, 0:768], in0=Cc, scalar1=-1.0, scalar2=None, op0=ALU.mult)
                OP(P, 'dve', 'memset', [], ['fbp'], fbp[:, 768:780], 0.0)

            def attend_sample(r, branch):
                poolk, poolv = (cak, cav) if branch == 'A' else (cbk, cbv)
                kTs_, vs_ = (kaTs, vas) if branch == 'A' else (kbTs, vbs)
                kself, vself = ('ska', 'sva') if branch == 'A' else ('skb', 'svb')
                ch0 = 0 if branch == 'A' else 12
                accs = [(pa[0], 'pa0'), (pa[1], 'pa1'), (pm[2], 'pm2'), (pm[3], 'pm3')]
                for pg in range(65):
                    if pg < 64:
                        k = pg % 2
                        P.idma(Kpg[k], poolk, rowi[:, pg:pg + 1], reads=['rowi'], writes=['Kpg%d' % k])
                        P.idma(Vpg[k], poolv, rowi[:, pg:pg + 1], reads=['rowi'], writes=['Vpg%d' % k])
                        OP(P, 'dve', 'tensor_copy', ['Kpg%d' % k], ['kbfp'], out=kbfp, in_=Kpg[k])
                        c = cnt['pt']
                        cnt['pt'] += 8 - (c % 8) if (c % 8) > 4 else 0
                        bank = (cnt['pt'] // 8) % 2
                        s0 = cnt['pt'] % 8
                        cnt['pt'] += 4
                        for n in range(4):
                            OP(P, 'pe', 'transpose', ['kbfp', 'ident_b'], ['pt%d' % bank], out=pt[bank][:, (s0 + n) * 128:(s0 + n + 1) * 128],
                               in_=kbfp[:, n * 128:(n + 1) * 128], identity=ident_b[:, :])
                        OP(P, 'act', 'copy', ['pt%d' % bank], ['kTp%d' % k], out=kTp[k], in_=pt[bank][:, s0 * 128:(s0 + 4) * 128])
                        OP(P, 'dve', 'tensor_copy', ['Vpg%d' % k, 'vones'], ['vbfp%d' % k], out=vbfp[k].rearrange("p (n c) -> p n c", c=130)[:, :, 0:128],
                           in_=Vpg[k].rearrange("p (n c) -> p n c", c=128))
                        kt_, kk_, vt_, vk_ = kTp[k], 'kTp%d' % k, vbfp[k], 'vbfp%d' % k
                    else:
                        kt_, kk_, vt_, vk_ = kTs_, kself, vs_, vself
                    pb = nxt('pmS', 2)
                    for n in range(4):
                        OP(P, 'pe', 'matmul', [kk_, 'qT'], ['pm%d' % pb], pm[pb][:, 3 * n:3 * n + 3], lhsT=kt_[:, n * 128:(n + 1) * 128],
                           rhs=qT[:, 3 * n:3 * n + 3, r], start=True, stop=True)
                    if branch == 'A':
                        OP(P, 'act', 'activation', ['pm%d' % pb], ['PTs'], out=PTs, in_=pm[pb][:, 0:12], func=AF.Exp, scale=SM_SCALE)
                        OP(P, 'dve', 'tensor_scalar', ['PTs', 'sel'], ['PTs'], out=PTs, in0=PTs, scalar1=sel[:, pg:pg + 1], scalar2=None, op0=ALU.mult)
                    else:
                        OP(P, 'act', 'activation', ['pm%d' % pb], ['Sx'], out=Sx, in_=pm[pb][:, 0:12], func=AF.Copy, scale=SM_SCALE)
                        OP(P, 'dve', 'tensor_tensor', ['Sx', 'fbp'], ['Sx'], out=Sx, in0=Sx, in1=fbp[:, pg * 12:(pg + 1) * 12], op=ALU.add)
                        OP(P, 'act', 'activation', ['Sx'], ['PTs'], out=PTs, in_=Sx, func=AF.Exp)
                        if pg == 64:
                            OP(P, 'dve', 'tensor_scalar', ['PTs', 'smc'], ['PTs'], out=PTs, in0=PTs, scalar1=OH(r), scalar2=None, op0=ALU.mult)
                    for n in range(4):
                        at, ak = accs[n]
                        OP(P, 'pe', 'matmul', ['PTs', vk_, 'vones'], [ak], at[0:3, 0:129], lhsT=PTs[:, 3 * n:3 * n + 3],
                           rhs=vt_[:, n * 130:n * 130 + 129], start=(pg == 0), stop=(pg == 64))
                for n in range(4):
                    at, ak = accs[n]
                    OP(P, 'act', 'copy', [ak], ['osm'], out=osm[0:3, n * 130:n * 130 + 129], in_=at[0:3, 0:129])
                for n in range(4):
                    OP(P, 'dve', 'reciprocal', ['osm'], ['rcp'], out=rcp[0:3, 0:1], in_=osm[0:3, n * 130 + 128:n * 130 + 129])
                    OP(P, 'dve', 'tensor_scalar', ['osm', 'rcp'], ['onbs'], out=onbs[0:3, n * 128:(n + 1) * 128], in0=osm[0:3, n * 130:n * 130 + 128],
                       scalar1=rcp[0:3, 0:1], scalar2=None, op0=ALU.mult)
                    c = cnt['pt']
                    cnt['pt'] += 1
                    bank, slot = (c // 8) % 2, c % 8
                    OP(P, 'pe', 'transpose', ['onbs', 'ident_b'], ['pt%d' % bank], out=pt[bank][:, slot * 128:slot * 128 + 3],
                       in_=onbs[0:3, n * 128:(n + 1) * 128], identity=ident_b[0:3, 0:3])
                    OP(P, 'act', 'copy', ['pt%d' % bank], ['ozT'], out=ozT[:, ch0 + 3 * n:ch0 + 3 * n + 3, r], in_=pt[bank][:, slot * 128:slot * 128 + 3])

            def attend_sample_c(r):
                for mb in range(2):
                    for (src, isk) in ((cmk, True), (cmv, False)):
                        for hf in range(2):
                            k = hf
                            P.dma('sp', Kpg[k], src[r * N_MEM + mb * 128:r * N_MEM + (mb + 1) * 128, hf * 512:(hf + 1) * 512], writes=['Kpg%d' % k])
                            if isk:
                                OP(P, 'dve', 'tensor_copy', ['Kpg%d' % k], ['kbfp'], out=kbfp, in_=Kpg[k])
                                for n in range(4):
                                    transpose_out(kbfp[:, n * 128:(n + 1) * 128], 'kbfp', 128, mkT[:, hf * 4 + n, mb * 128:(mb + 1) * 128], 'mkT')
                            else:
                                for hh in range(2):
                                    OP(P, 'dve', 'tensor_copy', ['Kpg%d' % k], ['mv'], out=mv[:, mb, hf * 2 + hh, 0:256], in_=Kpg[k][:, hh * 256:(hh + 1) * 256])
                accs = [(pa[0], 'pa0'), (pa[1], 'pa1'), (pm[2], 'pm2'), (pm[3], 'pm3')]
                for h in range(4):
                    pb = nxt('pmS', 2)
                    for mb in range(2):
                        for dc in range(2):
                            OP(P, 'pe', 'matmul', ['mkT', 'qT'], ['pm%d' % pb], pm[pb][:, mb:mb + 1], lhsT=mkT[:, 2 * h + dc, mb * 128:(mb + 1) * 128],
                               rhs=qT[:, 2 * h + dc, r:r + 1], start=(dc == 0), stop=(dc == 1))
                    OP(P, 'act', 'activation', ['pm%d' % pb], ['PTs'], out=PTs[:, 0:2], in_=pm[pb][:, 0:2], func=AF.Exp, scale=SM_SCALE_C)
                    at, ak = accs[h]
                    for mb in range(2):
                        OP(P, 'pe', 'matmul', ['PTs', 'mv'], [ak], at[0:1, 0:257], lhsT=PTs[:, mb:mb + 1], rhs=mv[:, mb, h, 0:257],
                           start=(mb == 0), stop=(mb == 1))
                    OP(P, 'act', 'copy', [ak], ['osm'], out=osm[0:1, 0:257], in_=at[0:1, 0:257])
                    OP(P, 'dve', 'reciprocal', ['osm'], ['rcp'], out=rcp[0:1, 0:1], in_=osm[0:1, 256:257])
                    OP(P, 'dve', 'tensor_scalar', ['osm', 'rcp'], ['onbs'], out=onbs[0:1, 0:256], in0=osm[0:1, 0:256], scalar1=rcp[0:1, 0:1],
                       scalar2=None, op0=ALU.mult)
                    for dc in range(2):
                        c = cnt['pt']
                        cnt['pt'] += 1
                        bank, slot = (c // 8) % 2, c % 8
                        OP(P, 'pe', 'transpose', ['onbs', 'ident_b'], ['pt%d' % bank], out=pt[bank][:, slot * 128:slot * 128 + 1],
                           in_=onbs[0:1, dc * 128:(dc + 1) * 128], identity=ident_b[0:1, 0:1])
                        OP(P, 'act', 'copy', ['pt%d' % bank], ['ozT'], out=ozT[:, 24 + 2 * h + dc, r:r + 1], in_=pt[bank][:, slot * 128:slot * 128 + 1])

            nseq = DBG.get('nseq', SAMP_PER_CORE)
            sphases = DBG.get('sphases', 'ABCDE')
            if 'A' in sphases:
                proj_q_tm('qa', 12, qT, 'qT', [(0, 16, (0, 16), (16, 32))])
                proj_q_tm('iq', 8, iqT, 'iqT', [(0, 8, (32, 40), (40, 48)), (64, 8, (32, 40), (40, 48))])
                wi = wload([(w_in, SEGS['iw'][0], 16, 0)], KC)
                pb = mm_tm(wi, 0, 16)
                OP(P, 'act', 'activation', ['pm%d' % pb], ['iw'], out=iw[:, 0, :], in_=pm[pb][:, 0:16], func=AF.Copy,
                   scale=(64 ** -0.5) * (16 ** -0.5))
                for r in range(nseq):
                    rowidx(r)
                    bcast_rows(r)
                    index_scores(r)
                    attend_sample(r, 'A')
                gate_z('za', 0, 12)
            if 'B' in sphases:
                q_fm('qb', 12)
                for r in range(nseq):
                    rowidx(r)
                    bcast_rows(r)
                    fox_bias(r)
                    attend_sample(r, 'B')
                gate_z('zb', 12, 12)
            if 'C' in sphases:
                q_fm('qc', 8)
                for r in range(nseq):
                    attend_sample_c(r)
                gate_z('zc', 24, 8)
            if DBG.get('dump') == 's':
                for c in range(32):
                    o = nxt('ost', 3)
                    OP(P, 'act', 'copy', ['ozT'], ['ost%d' % o], out=ost[o][:, 0:T], in_=ozT[:, c, :])
                    P.dma('act', odbg[c * 128:(c + 1) * 128, :], ost[o][:, 0:T], reads=['ost%d' % o], is_output=True)
            if 'D' in sphases:
                phase_D()
            if 'E' in sphases:
                phase_E(0, xsm, oys)

        P.finish()
        P.emit()
    return nc


def _rot_tables(pos, rot):
    half = rot // 2
    inv = np.power(np.float32(ROPE_THETA), -np.arange(half, dtype=np.float32) * np.float32(2.0 / rot)).astype(np.float32)
    ang = pos.astype(np.float32)[:, None] * inv[None, :]
    return np.cos(ang).astype(np.float32), np.sin(ang).astype(np.float32)


def _rot_pack(pos):
    nb = pos.shape[0] // 128
    c16, s16 = _rot_tables(pos, 32)
    c8, s8 = _rot_tables(pos, 16)
    tab = np.concatenate([c16, s16, c8, s8], axis=1).astype(np.float32)
    return np.ascontiguousarray(tab.reshape(nb, 128, 48).transpose(1, 0, 2).reshape(128, nb * 48))


def make_shared(inp):
    f32 = np.float32
    sh = {}
    sh['w_in'] = np.ascontiguousarray(np.asarray(inp['w_in'], f32)[0])
    sh['w_bra'] = np.ascontiguousarray(np.asarray(inp['w_br_a'], f32)[0])
    sh['w_brb'] = np.ascontiguousarray(np.asarray(inp['w_br_b'], f32)[0])
    sh['w_brc'] = np.ascontiguousarray(np.asarray(inp['w_br_c'], f32)[0])
    sh['w_o'] = np.ascontiguousarray(np.asarray(inp['w_out'], f32)[0])
    sh['w_mem'] = np.ascontiguousarray(np.asarray(inp['w_mem_kv'], f32)[0])
    sh['gcols'] = np.ascontiguousarray(np.concatenate(
        [np.asarray(inp['g_norm'], f32)[0].reshape(KC, 128).T, np.asarray(inp['g_mem'], f32)[0].reshape(KC, 128).T], axis=1))
    sh['gfin'] = np.ascontiguousarray(np.broadcast_to(np.asarray(inp['g_final'], f32)[None, :], (128, D_MODEL)))
    sh['bfrow'] = np.ascontiguousarray(np.broadcast_to(np.asarray(inp['b_forget'], f32)[0][None, :], (128, 12)))
    p = np.arange(128)
    ident = np.eye(128, dtype=f32)
    U = (p[:, None] <= p[None, :]).astype(f32)
    E127 = np.zeros((128, 128), f32); E127[127, :] = 1
    E64 = np.zeros((128, 128), f32); E64[64, :] = 1
    sh['cst'] = np.ascontiguousarray(np.concatenate([ident, U, E127, E64], axis=1))
    sh['kposrow'] = np.ascontiguousarray(np.broadcast_to(np.arange(SEQ, dtype=f32)[None, :], (128, SEQ)))
    return sh


QBLOCKS = ([0, 3, 4, 7, 8, 11, 12, 15], [1, 2, 5, 6, 9, 10, 13, 14])


def make_core_inputs(c, xp_b, xsamp, mem_b, sh, pools=None):
    f32 = np.float32
    half = c % 2
    m = dict(sh)
    m['xk'] = np.ascontiguousarray(xp_b)
    G = QBLOCKS[half]
    m['xq'] = np.ascontiguousarray(np.concatenate([xp_b[g * 128:(g + 1) * 128] for g in G], axis=0))
    xsm = np.zeros((128, D_MODEL), f32)
    xsm[:SAMP_PER_CORE] = xsamp[c * SAMP_PER_CORE:(c + 1) * SAMP_PER_CORE]
    m['xsm'] = xsm
    m['xmem'] = np.ascontiguousarray(mem_b)
    m['rotk'] = _rot_pack(np.concatenate([np.arange(SEQ), np.full(128, PAST_LEN)]))
    qp = np.concatenate([g * 128 + np.arange(128) for g in G])
    m['rotq'] = _rot_pack(qp)
    p = np.arange(128)
    qpos = (np.asarray(G)[None, :] * 128 + p[:, None]).astype(f32)
    kq = np.minimum(TOPK, qpos + 1).astype(f32)
    kpos = (np.arange(NKB)[None, :] * 128 + p[:, None]).astype(f32)
    m['tabs'] = np.ascontiguousarray(np.concatenate([kq, kpos, qpos], axis=1))
    m['qrow'] = np.ascontiguousarray(np.broadcast_to(qp.astype(f32)[None, :], (128, 1024)))
    sel = np.zeros((128, NQB, NKB, 12), f32)
    for j in range(NQB):
        sel[:, j, G[j], :] = 1
    m['selbc'] = np.ascontiguousarray(sel.reshape(128, -1))
    if pools is not None:
        for k in ('cak', 'cav', 'cbk', 'cbv', 'cai', 'cbl'):
            m[k] = pools[k]
        s0 = c * SAMP_PER_CORE
        m['cmk'] = np.ascontiguousarray(pools['cmk'][s0:s0 + SAMP_PER_CORE].reshape(SAMP_PER_CORE * N_MEM, 1024))
        m['cmv'] = np.ascontiguousarray(pools['cmv'][s0:s0 + SAMP_PER_CORE].reshape(SAMP_PER_CORE * N_MEM, 1024))
        pt = np.asarray(pools['pt'])[s0:s0 + SAMP_PER_CORE].astype(np.int32).reshape(1, -1)
        m['ptb'] = np.ascontiguousarray(np.broadcast_to(pt, (128, SAMP_PER_CORE * 64)))
        smc = np.zeros((128, 4 * 128 + 128 + 8), f32)
        for r in range(4):
            smc[r, r * 128:(r + 1) * 128] = 1.0
            smc[r, 640 + r] = 1.0
            smc[:, 644 + r] = NEG
            smc[r, 644 + r] = 0.0
        smc[:, 512:640] = 1.0
        m['smc'] = smc
    return m


def make_pools(cache_a_k, cache_a_v, cache_a_idx, cache_b_k, cache_b_v, cache_b_logf, cache_mem_k, cache_mem_v, page_table):
    f32 = np.float32
    rows = N_POOL * 128
    return dict(cak=np.asarray(cache_a_k, f32).reshape(rows, 512), cav=np.asarray(cache_a_v, f32).reshape(rows, 512),
                cbk=np.asarray(cache_b_k, f32).reshape(rows, 512), cbv=np.asarray(cache_b_v, f32).reshape(rows, 512),
                cai=np.asarray(cache_a_idx, f32).reshape(rows, 64), cbl=np.asarray(cache_b_logf, f32).reshape(rows, 12),
                cmk=np.asarray(cache_mem_k, f32)[0].reshape(DEC_BATCH, N_MEM, 1024),
                cmv=np.asarray(cache_mem_v, f32)[0].reshape(DEC_BATCH, N_MEM, 1024), pt=np.asarray(page_table))


_NC_CACHE = {}


def kernel(x_prompt, x_sample, cache_a_k, cache_a_v, cache_a_idx, cache_b_k, cache_b_v, cache_b_logf,
           cache_mem_k, cache_mem_v, page_table, mem_prompt, g_norm, w_in, b_forget, w_br_a, w_br_b,
           w_br_c, w_out, g_mem, w_mem_kv, g_final):
    f32 = np.float32
    inp = dict(w_in=w_in, w_br_a=w_br_a, w_br_b=w_br_b, w_br_c=w_br_c, w_out=w_out, w_mem_kv=w_mem_kv, g_norm=g_norm,
               g_mem=g_mem, g_final=g_final, b_forget=b_forget)
    sh = make_shared(inp)
    xp = np.asarray(x_prompt, f32)
    xsamp = np.asarray(x_sample, f32).reshape(DEC_BATCH, D_MODEL)
    memp = np.asarray(mem_prompt, f32)
    if 'nc' not in _NC_CACHE:
        _NC_CACHE['nc'] = build_nc()
    nc = _NC_CACHE['nc']
    pools = make_pools(cache_a_k, cache_a_v, cache_a_idx, cache_b_k, cache_b_v, cache_b_logf, cache_mem_k, cache_mem_v, page_table)
    in_maps = [make_core_inputs(c, xp[c // 2], xsamp, memp[c // 2], sh, pools) for c in range(NCORES)]
    res = run_bass_kernel_spmd(nc, in_maps, core_ids=list(range(NCORES)))
    R = [{k: np.asarray(v) for k, v in r.items()} for r in res.results]
    y_prompt = np.zeros((BATCH, SEQ, D_MODEL), f32)
    for c in range(NCORES):
        for j, g in enumerate(QBLOCKS[c % 2]):
            y_prompt[c // 2, g * 128:(g + 1) * 128] = R[c]['oy'][j * 128:(j + 1) * 128]
    pk = np.stack([R[2 * b]['okv'] for b in range(BATCH)])
    sk = np.concatenate([R[c]['osamp'][:SAMP_PER_CORE] for c in range(NCORES)], axis=0)
    om = np.stack([R[2 * b]['omem'] for b in range(BATCH)])

    def pr(c0, n, shape):
        return np.ascontiguousarray(pk[:, :, c0:c0 + n]).reshape(shape).astype(f32)

    def sr(c0, n, shape):
        return np.ascontiguousarray(sk[:, c0:c0 + n]).reshape(shape).astype(f32)

    p_ak = pr(C_KA, 512, (1, BATCH, SEQ, 4, 128))
    p_av = pr(C_VA, 512, (1, BATCH, SEQ, 4, 128))
    p_ai = pr(C_IK, 64, (1, BATCH, SEQ, 64))
    p_bk = pr(C_KB, 512, (1, BATCH, SEQ, 4, 128))
    p_bv = pr(C_VB, 512, (1, BATCH, SEQ, 4, 128))
    p_bf = pr(C_FB, 12, (1, BATCH, SEQ, 12))
    p_mk = np.ascontiguousarray(om[:, :, :1024]).reshape(1, BATCH, N_MEM, 4, 256).astype(f32)
    p_mv = np.ascontiguousarray(om[:, :, 1024:]).reshape(1, BATCH, N_MEM, 4, 256).astype(f32)
    s_ak = sr(C_KA, 512, (1, DEC_BATCH, 1, 4, 128))
    s_av = sr(C_VA, 512, (1, DEC_BATCH, 1, 4, 128))
    s_ai = sr(C_IK, 64, (1, DEC_BATCH, 1, 64))
    s_bk = sr(C_KB, 512, (1, DEC_BATCH, 1, 4, 128))
    s_bv = sr(C_VB, 512, (1, DEC_BATCH, 1, 4, 128))
    s_bf = sr(C_FB, 12, (1, DEC_BATCH, 1, 12))
    y_sample = np.concatenate([R[c]['oys'][:SAMP_PER_CORE] for c in range(NCORES)], axis=0).reshape(DEC_BATCH, 1, D_MODEL).astype(f32)
    return (y_prompt, y_sample, p_ak, p_av, p_ai, p_bk, p_bv, p_bf, p_mk, p_mv,
            s_ak, s_av, s_ai, s_bk, s_bv, s_bf)
```
